# Optimizing a Trainium2 kernel written in Bass

```python
import jax, jax.numpy as jnp
from jax import lax
import numpy as np

D_MODEL = 2048
BATCH = 2
SEQ = 4096
DEPTH = 4

GRID_W = 64
RET_HEADS = 8
RET_DK = 128
RET_DV = 128
RET_CHUNK = 128
ROPE_THETA = 10000.0
NA_HEADS = 8
NA_DH = 128
NA_WIN_ROWS = 8
NA_WIN_COLS = 16
SC_WIDTH = 1024
SC_WIDTH_K = 3
CF_WIDTH = 1024
CF_WIDTH_K = 31
D_FF = 4 * D_MODEL
N_BRANCH = 4
EPS = 1e-6

IN_SPLITS = (RET_HEADS * RET_DK, RET_HEADS * RET_DK, RET_HEADS * RET_DV, RET_HEADS * RET_DV,
             NA_HEADS * NA_DH, NA_HEADS * NA_DH, NA_HEADS * NA_DH,
             SC_WIDTH, SC_WIDTH, SC_WIDTH,
             2 * CF_WIDTH,
             N_BRANCH * D_MODEL)
D_IN = (2 * RET_HEADS * RET_DK + 2 * RET_HEADS * RET_DV + 3 * NA_HEADS * NA_DH
        + 3 * SC_WIDTH + 2 * CF_WIDTH + N_BRANCH * D_MODEL)

kernel_name = "hybrid_gated_branch_encoder"


def rms_norm(x, g):
    xf = x.astype(jnp.float32)
    y = xf * lax.rsqrt(jnp.mean(xf * xf, axis=-1, keepdims=True) + EPS)
    return (y * g.astype(jnp.float32)).astype(x.dtype)


def layer_norm(x, g, b):
    xf = x.astype(jnp.float32)
    mu = jnp.mean(xf, axis=-1, keepdims=True)
    var = jnp.mean(jnp.square(xf - mu), axis=-1, keepdims=True)
    y = (xf - mu) * lax.rsqrt(var + EPS)
    return (y * g.astype(jnp.float32) + b.astype(jnp.float32)).astype(x.dtype)


def split_cols(z, sizes):
    outs, start = [], 0
    for s in sizes:
        outs.append(z[..., start:start + s])
        start += s
    return outs


def rope(x, pos):
    half = x.shape[-1] // 2
    inv = ROPE_THETA ** (-jnp.arange(half, dtype=jnp.float32) / half)
    ang = pos.astype(jnp.float32)[:, None] * inv[None, :]
    cos = jnp.cos(ang)[None, :, None, :]
    sin = jnp.sin(ang)[None, :, None, :]
    x1 = x[..., :half].astype(jnp.float32)
    x2 = x[..., half:].astype(jnp.float32)
    return jnp.concatenate([x1 * cos - x2 * sin, x1 * sin + x2 * cos], axis=-1)


def depthwise_conv(x, w):
    k = w.shape[0]
    return lax.conv_general_dilated(
        x, w[:, None, :].astype(x.dtype), (1,), [(k // 2, k // 2)],
        dimension_numbers=('NWC', 'WIO', 'NWC'), feature_group_count=x.shape[-1])


def retention_scan(q, k, v, log_gamma, include_diag):
    b, t, h, dk = q.shape
    dv = v.shape[-1]
    c = RET_CHUNK
    n = t // c
    qc = q.reshape(b, n, c, h, dk).transpose(1, 0, 3, 2, 4)
    kc = k.reshape(b, n, c, h, dk).transpose(1, 0, 3, 2, 4)
    vc = v.reshape(b, n, c, h, dv).transpose(1, 0, 3, 2, 4)
    j = jnp.arange(c, dtype=jnp.float32)
    diff = j[:, None] - j[None, :]
    mask = (diff >= 0) if include_diag else (diff > 0)
    inner_decay = jnp.where(mask[None], jnp.exp(log_gamma[:, None, None] * jnp.where(mask, diff, 0.0)[None]), 0.0)
    q_decay = jnp.exp(log_gamma[:, None] * (j[None, :] + 1.0))
    k_decay = jnp.exp(log_gamma[:, None] * (c - 1.0 - j[None, :]))
    chunk_decay = jnp.exp(log_gamma * c)

    def step(state, xs):
        qi, ki, vi = xs
        s = jnp.einsum('bhqd,bhkd->bhqk', qi, ki) * inner_decay
        inner = jnp.einsum('bhqk,bhkv->bhqv', s, vi)
        cross = jnp.einsum('bhqd,bhdv->bhqv', qi * q_decay[..., None], state)
        state = chunk_decay[:, None, None] * state + jnp.einsum('bhkd,bhkv->bhdv', ki * k_decay[..., None], vi)
        return state, inner + cross

    state0 = jnp.zeros((b, h, dk, dv), jnp.float32)
    _, out = lax.scan(step, state0, (qc, kc, vc))
    return out.transpose(1, 0, 3, 2, 4).reshape(b, t, h, dv)


def retention_branch(rq, rk, rv, rg, decay_logit_f, decay_logit_b, pos):
    b, t, _ = rq.shape
    q = rope(rq.reshape(b, t, RET_HEADS, RET_DK), pos)
    k = rope(rk.reshape(b, t, RET_HEADS, RET_DK), pos) * (RET_DK ** -0.5)
    v = rv.reshape(b, t, RET_HEADS, RET_DV).astype(jnp.float32)
    lg_f = jax.nn.log_sigmoid(decay_logit_f.astype(jnp.float32))
    lg_b = jax.nn.log_sigmoid(decay_logit_b.astype(jnp.float32))
    y_f = retention_scan(q, k, v, lg_f, True)
    y_b = jnp.flip(retention_scan(jnp.flip(q, 1), jnp.flip(k, 1), jnp.flip(v, 1), lg_b, False), 1)
    y = y_f + y_b
    mu = jnp.mean(y, axis=-1, keepdims=True)
    var = jnp.mean(jnp.square(y - mu), axis=-1, keepdims=True)
    y = ((y - mu) * lax.rsqrt(var + EPS)).reshape(b, t, RET_HEADS * RET_DV)
    return jax.nn.silu(rg) * y.astype(rg.dtype)


def neighborhood_attention(nq, nk, nv, rpb):
    b, t, _ = nq.shape
    rows = t // GRID_W
    kr = min(NA_WIN_ROWS, rows)
    kc = NA_WIN_COLS
    qg = nq.reshape(b, rows, GRID_W, NA_HEADS, NA_DH)
    kg = nk.reshape(b, rows, GRID_W, NA_HEADS, NA_DH)
    vg = nv.reshape(b, rows, GRID_W, NA_HEADS, NA_DH)
    cols = jnp.arange(GRID_W)
    col_start = jnp.clip(cols - kc // 2, 0, GRID_W - kc)
    col_idx = col_start[:, None] + jnp.arange(kc)[None, :]
    dc = col_idx - cols[:, None] + (NA_WIN_COLS - 1)
    rpb32 = rpb.astype(jnp.float32)
    scale = NA_DH ** -0.5

    def one_row(r):
        row_start = jnp.clip(r - kr // 2, 0, rows - kr)
        q_r = lax.dynamic_index_in_dim(qg, r, axis=1, keepdims=False)
        k_rows = lax.dynamic_slice_in_dim(kg, row_start, kr, axis=1)
        v_rows = lax.dynamic_slice_in_dim(vg, row_start, kr, axis=1)
        k_win = k_rows[:, :, col_idx]
        v_win = v_rows[:, :, col_idx]
        dr = row_start + jnp.arange(kr) - r + (NA_WIN_ROWS - 1)
        bias = rpb32[:, dr[:, None, None], dc[None, :, :]]
        s = jnp.einsum('bqhd,brqkhd->bhqrk', q_r, k_win).astype(jnp.float32) * scale
        s = s + bias.transpose(0, 2, 1, 3)[None]
        p = jax.nn.softmax(s.reshape(b, NA_HEADS, GRID_W, kr * kc), axis=-1)
        p = p.reshape(b, NA_HEADS, GRID_W, kr, kc).astype(v_win.dtype)
        return jnp.einsum('bhqrk,brqkhd->bqhd', p, v_win)

    out = lax.map(one_row, jnp.arange(rows))
    return out.transpose(1, 0, 2, 3, 4).reshape(b, t, NA_HEADS * NA_DH)


def hybrid_mixer(h, w_in, dec_f, dec_b, rpb, sc_w, cf_w, cf_g, cf_b, w_ret_o, w_na_o, w_sc_o, w_cf_o, w_o, pos):
    b, t, _ = h.shape
    z = jnp.einsum('btd,de->bte', h, w_in)
    rq, rk, rv, rg, nq, nk, nv, s_b, s_c, s_x, c_glu, g_br = split_cols(z, IN_SPLITS)
    y_ret = retention_branch(rq, rk, rv, rg, dec_f, dec_b, pos)
    y_na = neighborhood_attention(nq, nk, nv, rpb)
    y_sc = s_b * depthwise_conv(s_c * s_x, sc_w)
    a, g = jnp.split(c_glu, 2, axis=-1)
    y_cf = jax.nn.silu(layer_norm(depthwise_conv(a * jax.nn.sigmoid(g), cf_w), cf_g, cf_b))
    gates = jax.nn.sigmoid(g_br.reshape(b, t, N_BRANCH, D_MODEL))
    merged = (gates[:, :, 0] * jnp.einsum('btk,kd->btd', y_ret, w_ret_o)
              + gates[:, :, 1] * jnp.einsum('btk,kd->btd', y_na, w_na_o)
              + gates[:, :, 2] * jnp.einsum('btk,kd->btd', y_sc, w_sc_o)
              + gates[:, :, 3] * jnp.einsum('btk,kd->btd', y_cf, w_cf_o))
    return jnp.einsum('btd,de->bte', merged, w_o)


def setup_inputs(seed: int = 0) -> dict:
    key = jax.random.key(seed)
    ks = jax.random.split(key, 24)
    f32 = jnp.float32
    nrm = lambda k, shape, s: jax.random.normal(k, shape, f32) * s
    base_logit = jnp.log(2.0 ** (5.0 + jnp.arange(RET_HEADS, dtype=f32)) - 1.0)
    return {
        "x": nrm(ks[0], (BATCH, SEQ, D_MODEL), 1.0),
        "c": nrm(ks[1], (BATCH, D_MODEL), 1.0),
        "w_ada": nrm(ks[2], (DEPTH, D_MODEL, 6 * D_MODEL), D_MODEL ** -0.5),
        "b_ada": nrm(ks[3], (DEPTH, 6 * D_MODEL), 0.02),
        "g_pre_mix": 1.0 + nrm(ks[4], (DEPTH, D_MODEL), 0.05),
        "g_post_mix": 1.0 + nrm(ks[5], (DEPTH, D_MODEL), 0.05),
        "g_pre_mlp": 1.0 + nrm(ks[6], (DEPTH, D_MODEL), 0.05),
        "g_post_mlp": 1.0 + nrm(ks[7], (DEPTH, D_MODEL), 0.05),
        "w_in": nrm(ks[8], (DEPTH, D_MODEL, D_IN), D_MODEL ** -0.5),
        "ret_decay_fwd": base_logit[None] + nrm(ks[9], (DEPTH, RET_HEADS), 0.1),
        "ret_decay_bwd": base_logit[None] + nrm(ks[10], (DEPTH, RET_HEADS), 0.1),
        "na_rpb": nrm(ks[11], (DEPTH, NA_HEADS, 2 * NA_WIN_ROWS - 1, 2 * NA_WIN_COLS - 1), 0.1),
        "sc_conv": nrm(ks[12], (DEPTH, SC_WIDTH_K, SC_WIDTH), SC_WIDTH_K ** -0.5),
        "cf_conv": nrm(ks[13], (DEPTH, CF_WIDTH_K, CF_WIDTH), CF_WIDTH_K ** -0.5),
        "cf_ln_g": 1.0 + nrm(ks[14], (DEPTH, CF_WIDTH), 0.05),
        "cf_ln_b": nrm(ks[15], (DEPTH, CF_WIDTH), 0.02),
        "w_ret_o": nrm(ks[16], (DEPTH, RET_HEADS * RET_DV, D_MODEL), (RET_HEADS * RET_DV) ** -0.5),
        "w_na_o": nrm(ks[17], (DEPTH, NA_HEADS * NA_DH, D_MODEL), (NA_HEADS * NA_DH) ** -0.5),
        "w_sc_o": nrm(ks[18], (DEPTH, SC_WIDTH, D_MODEL), SC_WIDTH ** -0.5),
        "w_cf_o": nrm(ks[19], (DEPTH, CF_WIDTH, D_MODEL), CF_WIDTH ** -0.5),
        "w_o": nrm(ks[20], (DEPTH, D_MODEL, D_MODEL), D_MODEL ** -0.5),
        "w_ff1": nrm(ks[21], (DEPTH, D_MODEL, D_FF), D_MODEL ** -0.5),
        "w_ff2": nrm(ks[22], (DEPTH, D_FF, D_MODEL), D_FF ** -0.5),
    }


def reference(x, c, w_ada, b_ada, g_pre_mix, g_post_mix, g_pre_mlp, g_post_mlp, w_in,
              ret_decay_fwd, ret_decay_bwd, na_rpb, sc_conv, cf_conv, cf_ln_g, cf_ln_b,
              w_ret_o, w_na_o, w_sc_o, w_cf_o, w_o, w_ff1, w_ff2):
    t = x.shape[1]
    pos = jnp.arange(t, dtype=jnp.int32)
    c_act = jax.nn.silu(c)
    for l in range(DEPTH):
        mod = jnp.einsum('bd,de->be', c_act, w_ada[l]) + b_ada[l]
        sh1, sc1, ga1, sh2, sc2, ga2 = jnp.split(mod[:, None, :], 6, axis=-1)
        h = rms_norm(x, g_pre_mix[l]) * (1 + sc1) + sh1
        y = hybrid_mixer(h, w_in[l], ret_decay_fwd[l], ret_decay_bwd[l], na_rpb[l], sc_conv[l], cf_conv[l],
                         cf_ln_g[l], cf_ln_b[l], w_ret_o[l], w_na_o[l], w_sc_o[l], w_cf_o[l], w_o[l], pos)
        x = x + ga1 * rms_norm(y, g_post_mix[l])
        h = rms_norm(x, g_pre_mlp[l]) * (1 + sc2) + sh2
        u = jnp.square(jax.nn.relu(jnp.einsum('btd,df->btf', h, w_ff1[l])))
        y = jnp.einsum('btf,fd->btd', u, w_ff2[l])
        x = x + ga2 * rms_norm(y, g_post_mlp[l])
    return x
```

```python
import numpy as np
import concourse.bass as bass
import concourse.mybir as mybir
from concourse.bass_utils import run_bass_kernel_spmd

F32 = mybir.dt.float32
BF16 = mybir.dt.bfloat16
U8 = mybir.dt.uint8
AF = mybir.ActivationFunctionType
ALU = mybir.AluOpType

NCORES = 8
DEPTH = 4
D = 2048
KC = 16
T = 1024
HAL = 256
TE = T + 2 * HAL
DIN = 20480
EPS = 1e-6
REG = 1024
ESZ = {F32: 4, BF16: 2, U8: 1}
NA_SCALE = 128 ** -0.5
NEG = -30000.0


class Buf:
    __slots__ = ("name", "w", "r")

    def __init__(self, name):
        self.name = name
        self.w = {}
        self.r = {}


class V:
    __slots__ = ("ap", "regs")

    def __init__(self, ap, regs):
        self.ap = ap
        self.regs = regs


def _rng(shape, key):
    if not isinstance(key, tuple):
        key = (key,)
    key = key + (slice(None),) * (len(shape) - len(key))
    strides = []
    s = 1
    for n in reversed(shape):
        strides.append(s)
        s *= n
    strides = strides[::-1]
    lo = 0
    hi = 0
    for k, n, st in zip(key, shape, strides):
        if isinstance(k, int):
            a, b = k, k + 1
        else:
            a = 0 if k.start is None else k.start
            b = n if k.stop is None else k.stop
        assert 0 <= a < b <= n, (shape, key)
        lo += a * st
        hi += (b - 1) * st
    return lo, hi + 1, key


class Tl:
    def __init__(self, prog, space, off, shape, dt, parts=128):
        self.prog, self.space, self.off, self.shape, self.dt = prog, space, off, tuple(shape), dt
        self.esz = ESZ[dt]
        n = int(np.prod(shape)) * self.esz
        self.nbytes = n
        base = prog.arena if space == "sb" else prog.psum
        gran = REG if space == "sb" else 2048
        self.gran = gran
        if space == "sb":
            ap = base[0:parts, off:off + n].bitcast(dt)
        else:
            ap = base[0:parts, off // 4:(off + n) // 4]
            if dt != F32:
                ap = ap.bitcast(dt)
        if len(shape) == 2:
            ap = ap.rearrange("p (a b) -> p a b", b=shape[1])
        elif len(shape) == 3:
            ap = ap.rearrange("p (a b c) -> p a b c", b=shape[1], c=shape[2])
        self.ap = ap

    def _regs(self, lo_e, hi_e):
        lo = self.off + lo_e * self.esz
        hi = self.off + hi_e * self.esz
        return [self.prog.buf((self.space, r)) for r in range(lo // self.gran, (hi - 1) // self.gran + 1)]

    @property
    def regs(self):
        return self._regs(0, int(np.prod(self.shape)))

    def __getitem__(self, key):
        lo, hi, key = _rng(self.shape, key)
        return V(self.ap[(slice(None),) + key], self._regs(lo, hi))

    def pv(self, p0, p1, key=()):
        lo, hi, key = _rng(self.shape, key)
        return V(self.ap[(slice(p0, p1),) + key], self._regs(lo, hi))


class Dr:
    def __init__(self, prog, name, shape, dt, kind="Internal"):
        self.prog, self.name, self.shape = prog, name, tuple(shape)
        self.t = prog.nc.dram_tensor(name, list(shape), dt, kind=kind)
        self.dt = dt
        self.ap = self.t.ap()

    def __getitem__(self, key):
        if not isinstance(key, tuple):
            key = (key,)
        k0 = key[0]
        if isinstance(k0, int):
            idx = range(k0, k0 + 1)
        else:
            idx = range(k0.start or 0, self.shape[0] if k0.stop is None else k0.stop)
        return V(self.ap[key], [self.prog.buf(("d", self.name, i)) for i in idx])


def _ap(x):
    return x.ap if isinstance(x, (V, Tl)) else x


def _regs(x):
    return x.regs if isinstance(x, (V, Tl)) else []


class Prog:
    def __init__(self):
        self.nc = bass.Bass("TRN2", target_bir_lowering=False)
        self.ops = []
        self._bufs = {}
        self.arena = None
        self.psum = None

    def buf(self, name):
        b = self._bufs.get(name)
        if b is None:
            b = self._bufs[name] = Buf(name)
        return b

    def _op(self, eng, fn, reads, writes, dma=False):
        r = []
        for x in reads:
            r += _regs(x)
        w = []
        for x in writes:
            w += _regs(x)
        self.ops.append((eng, fn, r, w, dma))

    def mm(self, out, lhsT, rhs, start, stop):
        o, l, r = _ap(out), _ap(lhsT), _ap(rhs)
        self._op("pe", lambda e: e.matmul(o, lhsT=l, rhs=r, start=start, stop=stop), [lhsT, rhs], [out])

    def tr(self, out, in_, ident):
        o, i, d = _ap(out), _ap(in_), _ap(ident)
        self._op("pe", lambda e: e.transpose(o, i, d), [in_, ident], [out])

    def act(self, out, in_, func, bias=None, scale=None, eng="act"):
        o, i = _ap(out), _ap(in_)
        kw = {}
        rd = [in_]
        if bias is not None:
            kw["bias"] = _ap(bias)
            rd.append(bias)
        if scale is not None:
            kw["scale"] = _ap(scale)
            rd.append(scale)
        self._op(eng, lambda e: e.activation(out=o, in_=i, func=func, **kw), rd, [out])

    def tt(self, eng, out, in0, in1, op):
        o, a, b = _ap(out), _ap(in0), _ap(in1)
        self._op(eng, lambda e: e.tensor_tensor(out=o, in0=a, in1=b, op=op), [in0, in1], [out])

    def ts(self, eng, out, in0, s1, s2, op0, op1=None):
        o, a = _ap(out), _ap(in0)
        a1, a2 = _ap(s1), _ap(s2)
        kw = {} if op1 is None else {"op1": op1}
        self._op(eng, lambda e: e.tensor_scalar(out=o, in0=a, scalar1=a1, scalar2=a2, op0=op0, **kw),
                 [in0, s1, s2], [out])

    def stt(self, eng, out, in0, scalar, in1, op0, op1):
        o, a, b, s = _ap(out), _ap(in0), _ap(in1), _ap(scalar)
        self._op(eng, lambda e: e.scalar_tensor_tensor(out=o, in0=a, scalar=s, in1=b, op0=op0, op1=op1),
                 [in0, in1, scalar], [out])

    def copy(self, eng, out, in_):
        o, i = _ap(out), _ap(in_)
        self._op(eng, lambda e: e.tensor_copy(out=o, in_=i), [in_], [out])

    def recip(self, out, in_):
        o, i = _ap(out), _ap(in_)
        self._op("dve", lambda e: e.reciprocal(out=o, in_=i), [in_], [out])

    def memset(self, eng, out, val):
        o = _ap(out)
        self._op(eng, lambda e: e.memset(o, val), [], [out])

    def dma(self, out, in_, q="sp"):
        o, i = _ap(out), _ap(in_)
        self._op(q, lambda e: e.dma_start(out=o, in_=i), [in_], [out], dma=True)

    def wait_all(self, eng, views):
        self._op(eng, None, views, [])

    def finalize(self):
        nc = self.nc
        ops = self.ops
        n = len(ops)
        deps = [None] * n
        sig = [False] * n
        KRING = 8
        dma_hist = {"sp": [], "pool": [], "act": []}
        for i, (eng, fn, r, w, dma) in enumerate(ops):
            d = set()
            for b in r:
                d.update(b.w.values())
                if b.name[0] == "ps":
                    d.update(v for k_, v in b.r.items() if k_ != eng)
            for b in w:
                d.update(b.w.values())
                d.update(b.r.values())
            d.discard(i)
            key = ("d", i) if dma else eng
            for b in r:
                b.r[key] = i
            for b in w:
                b.w = {key: i}
                b.r = {}
            if dma:
                h = dma_hist[eng]
                if len(h) >= KRING:
                    d.add(h[-KRING])
                h.append(i)
            dd = []
            for j in d:
                ej, _, _, _, dj = ops[j]
                if (not dj) and ej == eng and eng == "pe":
                    continue
                dd.append(j)
                sig[j] = True
            deps[i] = dd
        engs = {"pe": nc.tensor, "act": nc.scalar, "dve": nc.vector, "pool": nc.gpsimd, "sp": nc.sync}
        csem = {e: nc.alloc_semaphore(name="c_" + e) for e in ("pe", "act", "dve", "pool")}
        rings = {q: [nc.alloc_semaphore(name=f"d_{q}{k}") for k in range(KRING)] for q in ("sp", "pool")}
        cnt = {e: 0 for e in csem}
        dcnt = {q: 0 for q in rings}
        sv = [None] * n
        seen = {e: {} for e in engs}
        nwait = 0
        for i, (eng, fn, r, w, dma) in enumerate(ops):
            e = engs[eng]
            need = {}
            for j in deps[i]:
                s, v = sv[j]
                if need.get(s.num if hasattr(s, "num") else id(s), (None, -1))[1] < v:
                    need[s.num if hasattr(s, "num") else id(s)] = (s, v)
            for k, (s, v) in need.items():
                if seen[eng].get(k, -1) >= v:
                    continue
                seen[eng][k] = v
                e.wait_ge(s, v)
                nwait += 1
            if fn is None:
                continue
            ins = fn(e)
            if dma:
                q = dcnt[eng]
                dcnt[eng] += 1
                s = rings[eng][q % KRING]
                v = 16 * (q // KRING + 1)
                ins.then_inc(s, 16)
                sv[i] = (s, v)
            elif sig[i]:
                cnt[eng] += 1
                ins.then_inc(csem[eng], 1)
                sv[i] = (csem[eng], cnt[eng])
        print(f"[prog] ops={n} waits={nwait} sem_counts={cnt} dmas={dcnt}", flush=True)


class Builder:
    def __init__(self, kind):
        self.kind = kind
        self.P = P = Prog()
        nc = P.nc
        self.nc = nc
        self.outs = []
        self.in_names = []

        specs = {}
        if kind == "M":
            specs.update(ccol2=("ccol2", [128, KC, 2], F32), w_ada_s=("w_ada_s", [DEPTH, D, 1536], F32))
        else:
            specs.update(x_in=("xT", [128, KC, T], F32), modc=("modc", [128, 96], F32), b_ada=("b_ada", [128, 96], F32),
                         gvec=("gvec", [128, 4, KC], F32), dec=("dec", [2, 8], F32), cst=("cst", [128, 9, 128], F32),
                         rope=("rope", [128, 2, T], F32), pcst=("pcst", [128, 64], F32),
                         w_kv=("w_kv", [D, 2048], F32), w_in=("w_in", [D, DIN], F32),
                         nabias=("nabias", [8, 128, 27, 128], F32), scw=("scw", [128, 8, 3], F32),
                         cfw=("cfw", [128, 8, 31], F32), cfln=("cfln", [128, 2, 8], F32),
                         w_bo=("w_bo", [4, 1024, D], F32), w_o=("w_o", [D, D], F32), w_ff1=("w_ff1", [D, 4 * D], F32),
                         w_ff2=("w_ff2", [4 * D, D], F32), halo_in=("halo", [128, KC, 512], BF16),
                         L_in=("Lall", [8, 2, 8, 128, 128], F32))
        self._specs = specs
        if kind != "M":
            self.xT = nc.alloc_sbuf_tensor("xT_sb", [128, KC, T], F32)
            self.xbuf = [[P.buf(("x", k, h)) for h in range(2)] for k in range(KC)]
        free = nc.sbuf_bytes_remaining
        self.ASZ = (free - 2048) // 1024 * 1024
        P.arena = nc.alloc_sbuf_tensor("arena", [128, self.ASZ], U8)
        P.psum = nc.alloc_psum_tensor("psum", [128, 4096], F32)
        self._off = 0

    def __getattr__(self, name):
        specs = self.__dict__.get("_specs", {})
        if name in specs:
            nm, shape, dt = specs[name]
            ap = self.nc.dram_tensor(nm, list(shape), dt, kind="ExternalInput").ap()
            self.in_names.append(nm)
            self.__dict__[name] = ap
            return ap
        raise AttributeError(name)

    def alloc(self, shape, dt, parts=128):
        n = int(np.prod(shape)) * ESZ[dt]
        n_al = (n + REG - 1) // REG * REG
        off = self._off
        assert off + n_al <= self.ASZ, ("arena overflow", off, n_al, self.ASZ)
        self._off += n_al
        return Tl(self.P, "sb", off, shape, dt, parts)

    def mark(self):
        return self._off

    def release(self, m):
        self._off = m

    def pbank(self, b, shape=(512,), dt=F32, nb=1):
        return Tl(self.P, "ps", b * 2048, shape, dt)

    def xv(self, k, t0=0, n=T):
        regs = []
        if t0 < 512:
            regs.append(self.xbuf[k][0])
        if t0 + n > 512:
            regs.append(self.xbuf[k][1])
        return V(self.xT[:, k, t0:t0 + n], regs)

    def dump(self, name, view, shape, dt=F32):
        o = Dr(self.P, "dbg_" + name, shape, dt, kind="ExternalOutput")
        self.P.dma(V(o.ap, [self.P.buf(("d", "dbg_" + name, 0))]), view)
        self.outs.append(V(o.ap, [self.P.buf(("d", "dbg_" + name, 0))]))

    def setup(self):
        P = self.P
        self.C = self.alloc([9, 128], F32)
        P.dma(self.C, self.cst)
        self.ropeT = self.alloc([2, T], F32)
        P.dma(self.ropeT, self.rope)
        self.pc = self.alloc([64], F32)
        P.dma(self.pc, self.pcst)
        self.identb = self.alloc([128], BF16)
        self.onesb = self.alloc([128], BF16)
        P.copy("dve", self.identb, self.C[0])
        P.copy("dve", self.onesb, self.C[1])
        self.modT = self.alloc([96], F32)
        self.gv = self.alloc([4, KC], F32)
        self.der = self.alloc([6, KC], F32)
        self.lg = self.alloc([16], F32)
        self.gC = self.alloc([16], F32)
        self.DT = self.alloc([8, 128], F32)
        self.QD = self.alloc([2, 8, 128], F32)
        self.KDD = self.alloc([2, 8, 8], F32)
        self.coef = self.alloc([2, 8, 8], F32)
        self.wb = [self.alloc([KC, 512], BF16) for _ in range(2)]
        self.wi = 0
        self._pr = 0
        self.hT_mark = self.mark()
        self.hT = self.alloc([KC, T], BF16)
        self.base_mark = self.mark()
        for k in range(KC):
            P.dma(self.xv(k), self.x_in[:, k, :])

    def next_w(self):
        w = self.wb[self.wi % 2]
        self.wi += 1
        return w

    def load_w(self, src, nk=KC, cols=512, c0=0, w=None):
        if w is None:
            w = self.next_w()
        self.P.dma(w[0:nk, c0:c0 + cols], src.rearrange("(k p) c -> p k c", p=128), q="pool")
        return w

    def layer_tables(self):
        P = self.P
        m = self.mark()
        lgt = self.alloc([16], F32)
        P.dma(lgt, self.dec.rearrange("a h -> (a h)").partition_broadcast(128))
        e = self.alloc([16], F32)
        t = self.alloc([16], F32)
        P.act(e, lgt, AF.Exp, scale=-1.0)
        P.ts("dve", t, e, -0.2, 0.25, ALU.mult, ALU.add)
        for cst_ in (1.0 / 3, 0.5, 1.0):
            P.tt("dve", t, t, e, ALU.mult)
            P.ts("dve", t, t, -1.0, cst_, ALU.mult, ALU.add)
        P.tt("dve", t, t, e, ALU.mult)
        P.ts("dve", self.lg, t, -1.0, None, ALU.mult)
        P.act(self.gC, self.lg, AF.Exp, scale=128.0)
        t1 = self.alloc([128], F32)
        t2 = self.alloc([128], F32)
        for h in range(8):
            lf = self.lg[h:h + 1]
            lb = self.lg[8 + h:9 + h]
            P.act(t1, self.C[2], AF.Exp, scale=lf)
            P.act(t2, self.C[2], AF.Exp, scale=lb)
            P.tt("dve", t1, t1, self.C[3], ALU.mult)
            P.tt("dve", t2, t2, self.C[4], ALU.mult)
            P.tt("dve", self.DT[h], t1, t2, ALU.add)
            P.act(self.QD[0, h], self.C[5], AF.Exp, scale=lf)
            P.act(self.QD[1, h], self.C[6], AF.Exp, scale=lb)
            P.act(self.KDD[0, h], self.pc[34:42], AF.Exp, scale=lf)
            P.act(self.KDD[1, h], self.pc[42:50], AF.Exp, scale=lb)
            P.act(self.coef[0, h], self.pc[2:10], AF.Exp, scale=lf)
            P.act(self.coef[1, h], self.pc[18:26], AF.Exp, scale=lb)
            P.tt("dve", self.coef[0, h], self.coef[0, h], self.pc[10:18], ALU.mult)
            P.tt("dve", self.coef[1, h], self.coef[1, h], self.pc[26:34], ALU.mult)
        self.release(m)

    def mod(self):
        P = self.P
        m = self.mark()
        bc = self.alloc([96], F32)
        P.dma(bc, self.b_ada)
        P.dma(self.gv, self.gvec)
        mc = self.alloc([96], F32)
        P.dma(mc, self.modc)
        P.tt("dve", self.modT, mc, bc, ALU.add)
        P.ts("dve", self.der[4], self.modT[16:32], 1.0, None, ALU.add)
        P.tt("dve", self.der[0], self.der[4], self.gv[0], ALU.mult)
        P.ts("dve", self.der[5], self.modT[64:80], 1.0, None, ALU.add)
        P.tt("dve", self.der[1], self.der[5], self.gv[2], ALU.mult)
        P.tt("dve", self.der[2], self.modT[32:48], self.gv[1], ALU.mult)
        P.tt("dve", self.der[3], self.modT[80:96], self.gv[3], ALU.mult)
        self.release(m)

    def rstd_from_sq(self, src_fn, nk, inv_n, out_rstd, loader=None, ring=None):
        P = self.P
        m = self.mark()
        sq = [self.alloc([T], BF16) for _ in range(3)]
        pb = self.pbank(4, (2, 512))
        for k in range(nk):
            s = sq[k % 3]
            if loader is not None:
                t = ring()
                loader(k, t)
                P.act(s, t, AF.Square)
            else:
                P.act(s, src_fn(k), AF.Square)
            for th in range(2):
                P.mm(pb[th], self.onesb, s[th * 512:(th + 1) * 512], k == 0, k == nk - 1)
        P.ts("dve", out_rstd, pb, inv_n, EPS, ALU.mult, ALU.add)
        P.act(out_rstd, out_rstd, AF.Ln)
        P.act(out_rstd, out_rstd, AF.Exp, scale=-0.5)
        self.release(m)

    def norm_to_h(self, s_fn, sh_fn):
        P = self.P
        m = self.mark()
        rstd = self.alloc([T], F32)
        self.rstd_from_sq(lambda k: self.xv(k), KC, 1.0 / D, rstd)
        xr = [self.alloc([T], F32) for _ in range(2)]
        for k in range(KC):
            t = xr[k % 2]
            P.tt("dve", t, self.xv(k), rstd, ALU.mult)
            P.act(self.hT[k], t, AF.Identity, bias=sh_fn(k), scale=s_fn(k))
        self.release(m)


    def scratch(self, names):
        for name, shape, dt in names:
            setattr(self, name, Dr(self.P, name, shape, dt))

    def proj(self, w, nk, ccs, rhs_chunks, epi, banks=(0, 1, 2, 3)):
        P = self.P
        for cc in ccs:
            for ci, (rhs_fn, n) in enumerate(rhs_chunks):
                b = banks[self._pr % len(banks)]
                self._pr += 1
                pb = self.pbank(b, (512,))
                for k in range(nk):
                    P.mm(pb[0:n], w[k, cc * 128:(cc + 1) * 128], rhs_fn(k), k == 0, k == nk - 1)
                epi(cc, ci, pb[0:n], n)

    def own_chunks(self):
        return [((lambda k, t0=t0: self.hT[k, t0:t0 + 512]), 512) for t0 in (0, 512)]

    def ext_chunks(self):
        return [((lambda k: self.hH[k, 0:256]), 256),
                ((lambda k: self.hT[k, 0:512]), 512),
                ((lambda k: self.hT[k, 512:1024]), 512),
                ((lambda k: self.hH[k, 256:512]), 256)]

    def conv_chunks(self):
        return [((lambda k: self.hH[k, 240:256]), 16),
                ((lambda k: self.hT[k, 0:512]), 512),
                ((lambda k: self.hT[k, 512:1024]), 512),
                ((lambda k: self.hH[k, 256:272]), 16)]

    def ring(self, name, n, shape, dt):
        tiles = [self.alloc(shape, dt) for _ in range(n)]
        st = {"i": 0}

        def nxt():
            t = tiles[st["i"] % n]
            st["i"] += 1
            return t
        return nxt

    @staticmethod
    def vp(v, p0, p1):
        return V(v.ap[p0:p1], v.regs)

    def proj_rope(self, w, g2, dst, with_tok, scale=None):
        P = self.P
        m = self.mark()
        t1n = self.ring("t1", 2, [512], F32)
        t2n = self.ring("t2", 2, [512], F32)
        obn = self.ring("ob", 2, [512], BF16)
        ktn = self.ring("kt", 2, [4, 128], BF16)

        def epi(cc, ci, pb, n):
            h = g2 * 4 + cc
            t0 = ci * 512
            t1, t2, ob = t1n(), t2n(), obn()
            P.tt("dve", t1, pb, self.ropeT[0, t0:t0 + 512], ALU.mult)
            P.tt("dve", t2.pv(0, 64), self.vp(pb, 64, 128), self.ropeT.pv(0, 64, (1, slice(t0, t0 + 512))), ALU.mult)
            P.tt("dve", t2.pv(64, 128), self.vp(pb, 0, 64), self.ropeT.pv(64, 128, (1, slice(t0, t0 + 512))), ALU.mult)
            if scale is None:
                P.tt("pool", ob, t1, t2, ALU.add)
            else:
                P.tt("pool", t1, t1, t2, ALU.add)
                P.act(ob, t1, AF.Copy, scale=scale)
            P.dma(dst[h, :, t0:t0 + 512], ob)
            if with_tok:
                pt = self.pbank(4 + (self._pr % 2), (4, 128), BF16)
                for i in range(4):
                    P.tr(pt[i], ob[i * 128:(i + 1) * 128], self.identb)
                kt = ktn()
                P.act(kt, pt, AF.Copy)
                c0 = ci * 4
                P.dma(V(self.zK.ap[c0:c0 + 4, :, h * 128:(h + 1) * 128].rearrange("c j d -> j c d"),
                        self.zK[c0:c0 + 4].regs), kt)
        self.proj(w, KC, range(4), self.own_chunks(), epi)
        self.release(m)

    def proj_tok(self, w, srcs, dst, c0col):
        P = self.P
        m = self.mark()
        obn = self.ring("obt", 2, [512], BF16)
        for e, src in enumerate(srcs):
            b = self._pr % 4
            self._pr += 1
            pb = self.pbank(b, (512,))
            for k in range(KC):
                P.mm(pb, src(k), w[k, 0:512], k == 0, k == KC - 1)
            ob = obn()
            P.act(ob, pb, AF.Copy)
            P.dma(dst[e, :, c0col:c0col + 512], ob)
        self.release(m)

    def own_tok_srcs(self):
        return [(lambda k, c=c: self.hT[k, c * 128:(c + 1) * 128]) for c in range(8)]

    def ext_tok_srcs(self):
        r = []
        for e in range(12):
            if e < 2:
                r.append(lambda k, e=e: self.hH[k, e * 128:(e + 1) * 128])
            elif e < 10:
                r.append(lambda k, e=e: self.hT[k, (e - 2) * 128:(e - 1) * 128])
            else:
                r.append(lambda k, e=e: self.hH[k, 256 + (e - 10) * 128:256 + (e - 9) * 128])
        return r

    def proj_simple(self, w, chunks, func, dst_fn, eng_alt=True):
        P = self.P
        m = self.mark()
        obn = self.ring("obs", 3, [512], BF16)

        def epi(cc, ci, pb, n):
            ob = obn()
            if func is None and (self._pr % 2 == 0):
                P.copy("dve", ob[0:n], pb)
            else:
                P.act(ob[0:n], pb, AF.Copy if func is None else func)
            P.dma(dst_fn(cc, ci), ob[0:n])
        self.proj(w, KC, range(4), chunks, epi)
        self.release(m)

    def load_tok(self, t, src, h, nchunks):
        self.P.dma(t, V(src.ap[0:nchunks, :, h * 128:(h + 1) * 128].rearrange("c j d -> j c d"),
                        src[0:nchunks].regs))

    def l_phase(self, Lout):
        P = self.P
        m = self.mark()
        ktn = self.ring("lk", 2, [8, 128], BF16)
        vtn = self.ring("lv", 2, [8, 128], BF16)
        kdn = self.ring("lkd", 2, [8, 128], BF16)
        lsn = self.ring("ls", 2, [128], F32)
        n = 0
        for h in range(8):
            kt, vt = ktn(), vtn()
            self.load_tok(kt, self.zK, h, 8)
            self.load_tok(vt, self.zv, h, 8)
            for dr in range(2):
                kd = kdn()
                for c in range(8):
                    sc = self.KDD[dr, h, c:c + 1]
                    if c % 2 == 0:
                        P.ts("dve", kd[c], kt[c], sc, None, ALU.mult)
                    else:
                        P.act(kd[c], kt[c], AF.Copy, scale=sc)
                pb = self.pbank(n % 4, (128,))
                n += 1
                for c in range(8):
                    P.mm(pb, kd[c], vt[c], c == 0, c == 7)
                ls = lsn()
                P.act(ls, pb, AF.Copy)
                P.dma(V(Lout.ap[dr, h], [P.buf(("d", "Lout", dr * 8 + h))]), ls)
                self.outs.append(V(Lout.ap[dr, h], [P.buf(("d", "Lout", dr * 8 + h))]))
        self.release(m)

    def rstd_chain(self, out, src):
        P = self.P
        P.ts("dve", out, src, EPS, None, ALU.add)
        P.act(out, out, AF.Ln)
        P.act(out, out, AF.Exp, scale=-0.5)

    def ret_head(self, h):
        import os
        CUT = int(os.environ.get('RCUT', '99'))
        P = self.P
        m = self.mark()
        A = self.alloc
        qT, kT, sg = A([8, 128], BF16), A([8, 128], BF16), A([8, 128], BF16)
        kt, vt = A([8, 128], BF16), A([8, 128], BF16)
        P.dma(qT, self.zq[h])
        P.dma(kT, self.zk[h])
        P.dma(sg, self.zg[h])
        self.load_tok(kt, self.zK, h, 8)
        self.load_tok(vt, self.zv, h, 8)
        La = A([8, 2, 128], F32)
        for s_ in range(8):
            P.dma(La[s_], self.L_in[s_, :, h].rearrange("r d v -> d r v"))
        if CUT < 1:
            self.release(m); return
        Sf, Sb = A([8, 128], F32), A([8, 128], F32)
        for dr, S, c0 in ((0, Sf, 0), (1, Sb, 7)):
            P.ts("dve", S[c0], La[0, dr], self.coef[dr, h, 0:1], None, ALU.mult)
            for s_ in range(1, 8):
                P.stt("dve", S[c0], La[s_, dr], self.coef[dr, h, s_:s_ + 1], S[c0], ALU.mult, ALU.add)
        if CUT < 2:
            self.release(m); return
        kdf, kdb = A([8, 128], BF16), A([8, 128], BF16)
        P.ts("dve", kdf, kt, self.KDD[0, h, 7:8], None, ALU.mult)
        P.act(kdb, kt, AF.Copy, scale=self.KDD[1, h, 0:1])
        Uf = self.pbank(0, (8, 128))
        Ub = self.pbank(2, (8, 128))
        for c in range(8):
            P.mm(Uf[c], kdf[c], vt[c], True, True)
        for c in range(8):
            P.mm(Ub[c], kdb[c], vt[c], True, True)
        for c in range(7):
            P.stt("dve", Sf[c + 1], Sf[c], self.gC[h:h + 1], Uf[c], ALU.mult, ALU.add)
        for c in range(7, 0, -1):
            P.stt("pool" if False else "dve", Sb[c - 1], Sb[c], self.gC[8 + h:9 + h], Ub[c], ALU.mult, ALU.add)
        if CUT < 3:
            self.release(m); return
        Sfb, Sbb = A([8, 128], BF16), A([8, 128], BF16)
        P.act(Sfb, Sf, AF.Copy)
        P.act(Sbb, Sb, AF.Copy)
        qdf, qdb = A([8, 128], BF16), A([8, 128], BF16)
        for dr, qd in ((0, qdf), (1, qdb)):
            qv = self.QD[dr, h]
            P.tt("dve", qd, qT,
                 V(qv.ap.unsqueeze(1).to_broadcast([128, 8, 128]), qv.regs), ALU.mult)
        if CUT < 4:
            self.release(m); return
        St = self.pbank(4, (8, 128))
        for c in range(8):
            P.mm(St[c], kT[c], qT[c], True, True)
        PT = A([8, 128], BF16)
        dv = self.DT[h]
        P.tt("dve", PT, St, V(dv.ap.unsqueeze(1).to_broadcast([128, 8, 128]), dv.regs), ALU.mult)
        if CUT < 5:
            self.release(m); return
        O = self.pbank(6, (8, 128))
        for c in range(8):
            P.mm(O[c], vt[c], PT[c], True, False)
            P.mm(O[c], Sfb[c], qdf[c], False, False)
            P.mm(O[c], Sbb[c], qdb[c], False, True)
        if CUT < 6:
            self.release(m); return
        y = A([1024], F32)
        ybf, ysq = A([1024], BF16), A([1024], BF16)
        Of = self.pbank(6, (1024,))
        for th in range(2):
            hs = slice(th * 512, (th + 1) * 512)
            P.act(y[hs], Of[hs], AF.Copy)
            P.act(ysq[hs], Of[hs], AF.Square)
            P.copy("dve", ybf[hs], Of[hs])
        mp = self.pbank(0, (1024,))
        sp = self.pbank(2, (1024,))
        for th in range(2):
            P.mm(mp[th * 512:(th + 1) * 512], self.onesb, ybf[th * 512:(th + 1) * 512], True, True)
            P.mm(sp[th * 512:(th + 1) * 512], self.onesb, ysq[th * 512:(th + 1) * 512], True, True)
        if CUT < 7:
            self.release(m); return
        mean, t2 = A([1024], F32), A([1024], F32)
        P.ts("dve", mean, mp, 1.0 / 128, None, ALU.mult)
        P.tt("pool", t2, mean, mean, ALU.mult)
        P.stt("dve", t2, sp, 1.0 / 128, t2, ALU.mult, ALU.subtract)
        self.rstd_chain(t2, t2)
        P.tt("dve", y, y, mean, ALU.subtract)
        P.tt("pool", y, y, t2, ALU.mult)
        yo = A([1024], BF16)
        P.tt("dve", yo, y, V(sg.ap.rearrange("p c i -> p (c i)"), sg.regs), ALU.mult)
        P.dma(self.zy[h], yo)
        self.release(m)

    def na_head(self, h):
        P = self.P
        m = self.mark()
        A = self.alloc
        qT, kT, vt = A([8, 128], BF16), A([12, 128], BF16), A([12, 128], BF16)
        bias = A([27, 128], F32)
        P.dma(qT, self.znq[h])
        P.dma(kT, self.znk[h])
        self.load_tok(vt, self.znv, h, 12)
        P.dma(bias, self.nabias[h])
        tmpn = self.ring("natmp", 2, [6, 128], F32)
        ptn = self.ring("napt", 2, [6, 128], BF16)
        O = self.pbank(4, (8, 128))
        Dn = self.pbank(6, (8, 128))
        specs = [(-2, 6, 0), (-2, 5, 6), (-2, 5, 11), (-2, 5, 11), (-2, 5, 11), (-2, 5, 11), (-2, 5, 16), (-3, 6, 21)]
        for mq in range(8):
            d0, nt, ti0 = specs[mq]
            St = self.pbank(0 if mq % 2 == 0 else 2, (6, 128))
            for s_ in range(nt):
                e = mq + 2 + d0 + s_
                P.mm(St[s_], kT[e], qT[mq], True, True)
            tmp, PT = tmpn(), ptn()
            P.stt("dve", tmp[0:nt], St[0:nt], NA_SCALE, bias[ti0:ti0 + nt], ALU.mult, ALU.add)
            P.act(PT[0:nt], tmp[0:nt], AF.Exp)
            for s_ in range(nt):
                e = mq + 2 + d0 + s_
                P.mm(O[mq], vt[e], PT[s_], s_ == 0, s_ == nt - 1)
            for s_ in range(nt):
                P.mm(Dn[mq], self.onesb, PT[s_], s_ == 0, s_ == nt - 1)
        rd = A([8, 128], F32)
        P.recip(rd, Dn)
        yo = A([8, 128], BF16)
        P.tt("dve", yo, O, rd, ALU.mult)
        P.dma(self.zy[8 + h], yo)
        self.release(m)

    def proj_pairs(self, colA, colB, funcA, dst):
        P = self.P
        m = self.mark()
        tmn = self.ring("pp_t", 2, [512], F32)
        obn = self.ring("pp_o", 2, [512], BF16)
        i0s = (0, 16, 528, 1040)
        for j in range(4):
            w = self.next_w()
            P.dma(w[0:KC, 0:256], self.w_in[:, colA + j * 256:colA + (j + 1) * 256].rearrange("(k p) c -> p k c", p=128), q="pool")
            P.dma(w[0:KC, 256:512], self.w_in[:, colB + j * 256:colB + (j + 1) * 256].rearrange("(k p) c -> p k c", p=128), q="pool")
            for a in range(2):
                ch = 2 * j + a
                for ci, (rhs_fn, n) in enumerate(self.conv_chunks()):
                    pa = self.pbank(self._pr % 4, (512,))
                    pb = self.pbank((self._pr + 1) % 4, (512,))
                    self._pr += 2
                    for k in range(KC):
                        P.mm(pa[0:n], w[k, a * 128:(a + 1) * 128], rhs_fn(k), k == 0, k == KC - 1)
                    for k in range(KC):
                        P.mm(pb[0:n], w[k, 256 + a * 128:256 + (a + 1) * 128], rhs_fn(k), k == 0, k == KC - 1)
                    tm, ob = tmn(), obn()
                    P.act(tm[0:n], pa[0:n], funcA)
                    P.tt("dve", ob[0:n], tm[0:n], pb[0:n], ALU.mult)
                    if ci == 0:
                        P.ts("dve", ob[0:n], ob[0:n], self.pc[0:1], None, ALU.mult)
                    if ci == 3:
                        P.ts("dve", ob[0:n], ob[0:n], self.pc[1:2], None, ALU.mult)
                    P.dma(dst[ch, :, i0s[ci]:i0s[ci] + n], ob[0:n])
        self.release(m)

    def sc_branch(self):
        P = self.P
        m = self.mark()
        A = self.alloc
        wt = A([8, 3], F32)
        P.dma(wt, self.scw)
        pn = self.ring("sc_p", 2, [1056], BF16)
        sbn = self.ring("sc_sb", 2, [1024], BF16)
        dn = self.ring("sc_d", 2, [3, 128], BF16)
        yon = self.ring("sc_y", 2, [1024], BF16)
        for ch in range(8):
            p, sb, d3, yo = pn(), sbn(), dn(), yon()
            P.dma(p, self.zp[ch])
            P.dma(sb, self.zsb[ch])
            for k in range(3):
                P.ts("pool", d3[k], self.identb, wt[ch, k:k + 1], None, ALU.mult)
            for th in range(2):
                pb = self.pbank(self._pr % 4, (512,))
                self._pr += 1
                for k in range(3):
                    o = 16 + th * 512 + k - 1
                    P.mm(pb, d3[k], p[o:o + 512], k == 0, k == 2)
                P.tt("dve", yo[th * 512:(th + 1) * 512], sb[th * 512:(th + 1) * 512], pb, ALU.mult)
            P.dma(self.zy[16 + ch], yo)
        self.release(m)

    def cf_branch(self):
        P = self.P
        m = self.mark()
        A = self.alloc
        wt = A([8, 31], F32)
        P.dma(wt, self.cfw)
        ln = A([2, 8], F32)
        P.dma(ln, self.cfln)
        un = self.ring("cf_u", 2, [1056], BF16)
        d31 = A([31, 128], BF16)
        yc = A([8, 1024], F32)
        ybn = self.ring("cf_yb", 2, [1024], BF16)
        ysn = self.ring("cf_ys", 2, [1024], BF16)
        sm = self.pbank(4, (1024,))
        sq = self.pbank(6, (1024,))
        for ch in range(8):
            u = un()
            P.dma(u, self.zu[ch])
            for k in range(31):
                P.ts("pool" if k % 2 else "dve", d31[k], self.identb, wt[ch, k:k + 1], None, ALU.mult)
            yb, ys = ybn(), ysn()
            for th in range(2):
                pb = self.pbank(self._pr % 4, (512,))
                self._pr += 1
                for k in range(31):
                    o = 16 + th * 512 + k - 15
                    P.mm(pb, d31[k], u[o:o + 512], k == 0, k == 30)
                P.act(yc[ch, th * 512:(th + 1) * 512], pb, AF.Copy)
                P.act(ys[th * 512:(th + 1) * 512], pb, AF.Square)
                P.copy("dve", yb[th * 512:(th + 1) * 512], pb)
            for th in range(2):
                P.mm(sm[th * 512:(th + 1) * 512], self.onesb, yb[th * 512:(th + 1) * 512], ch == 0, ch == 7)
                P.mm(sq[th * 512:(th + 1) * 512], self.onesb, ys[th * 512:(th + 1) * 512], ch == 0, ch == 7)
        mean, t2 = A([1024], F32), A([1024], F32)
        P.ts("dve", mean, sm, 1.0 / 1024, None, ALU.mult)
        P.tt("pool", t2, mean, mean, ALU.mult)
        P.stt("dve", t2, sq, 1.0 / 1024, t2, ALU.mult, ALU.subtract)
        self.rstd_chain(t2, t2)
        yon = self.ring("cf_yo", 2, [1024], BF16)
        for ch in range(8):
            P.tt("dve", yc[ch], yc[ch], mean, ALU.subtract)
            P.tt("pool", yc[ch], yc[ch], t2, ALU.mult)
            yo = yon()
            P.act(yo, yc[ch], AF.Silu, bias=ln[1, ch:ch + 1], scale=ln[0, ch:ch + 1])
            P.dma(self.zy[24 + ch], yo)
        self.release(m)

    def merge(self):
        P = self.P
        m = self.mark()
        A = self.alloc
        acc = A([4, 1024], F32)
        yb = A([8, 1024], BF16)
        gn = self.ring("mg_g", 2, [1024], BF16)
        tn = self.ring("mg_t", 2, [512], F32)
        mg = A([4, 1024], BF16)
        for cg in range(4):
            for i in range(4):
                w = self.load_w(self.w_bo[i][:, cg * 512:(cg + 1) * 512], nk=8)
                P.dma(yb, V(self.zy.ap[i * 8:(i + 1) * 8].rearrange("c p t -> p c t"), self.zy[i * 8:(i + 1) * 8].regs))
                for cc in range(4):
                    g = gn()
                    P.dma(g, self.zgate[i * 16 + cg * 4 + cc])
                    for th in range(2):
                        pb = self.pbank(self._pr % 4, (512,))
                        self._pr += 1
                        for kk in range(8):
                            P.mm(pb, w[kk, cc * 128:(cc + 1) * 128], yb[kk, th * 512:(th + 1) * 512], kk == 0, kk == 7)
                        a = acc[cc, th * 512:(th + 1) * 512]
                        gv = g[th * 512:(th + 1) * 512]
                        if i == 0:
                            P.tt("dve", a, gv, pb, ALU.mult)
                        else:
                            t = tn()
                            P.tt("dve", t, gv, pb, ALU.mult)
                            P.tt("pool", a, a, t, ALU.add)
            P.act(mg, acc, AF.Copy)
            P.dma(self.zmg[cg], mg)
        self.release(m)
        for cg in range(4):
            P.dma(self.hT[cg * 4:(cg + 1) * 4], self.zmg[cg])

    def dense_to_zyo(self, wsrc_fn, ngroups=4):
        P = self.P
        m = self.mark()
        yon = self.ring("wo_y", 3, [512], F32)
        for cg in range(ngroups):
            w = self.load_w(wsrc_fn(cg))

            def epi(cc, ci, pb, n, cg=cg):
                yo = yon()
                P.act(yo, pb, AF.Copy)
                P.dma(self.zyo[cg * 4 + cc, :, ci * 512:(ci + 1) * 512], yo)
            self.proj(w, KC, range(4), self.own_chunks(), epi)
        self.release(m)

    def postnorm_residual(self, gidx):
        P = self.P
        m = self.mark()
        yn = self.ring("pn_y", 2, [1024], F32)
        rstd = self.alloc([1024], F32)
        self.rstd_from_sq(None, KC, 1.0 / D, rstd, loader=lambda k, t: P.dma(t, self.zyo[k]), ring=yn)
        tn = self.ring("pn_t", 2, [1024], F32)
        for j in range(KC):
            y = yn()
            P.dma(y, self.zyo[j])
            t = tn()
            P.tt("dve", t, y, rstd, ALU.mult)
            P.stt("dve", self.xv(j), t, self.der[gidx, j:j + 1], self.xv(j), ALU.mult, ALU.add)
        self.release(m)

    def mlp(self):
        P = self.P
        self.norm_to_h(lambda k: self.der[1, k:k + 1], lambda k: self.modT[48 + k:49 + k])
        m = self.mark()
        tn = self.ring("ff_t", 2, [512], F32)
        un = self.ring("ff_u", 3, [512], BF16)
        for g in range(16):
            w = self.load_w(self.w_ff1[:, g * 512:(g + 1) * 512])

            def epi(cc, ci, pb, n, g=g):
                t, u = tn(), un()
                P.act(t, pb, AF.Relu)
                P.tt("dve" if (cc + ci) % 2 else "pool", u, t, t, ALU.mult)
                P.dma(self.zuu[g * 4 + cc, :, ci * 512:(ci + 1) * 512], u)
            self.proj(w, KC, range(4), self.own_chunks(), epi)
        self.release(m)
        m = self.mark()
        upn = self.ring("ff_up", 2, [8, 1024], BF16)
        yon = self.ring("ff_y", 2, [512], F32)
        for cg in range(4):
            for pc_ in range(8):
                w = self.load_w(self.w_ff2[pc_ * 1024:(pc_ + 1) * 1024, cg * 512:(cg + 1) * 512], nk=8)
                up = upn()
                P.dma(up, V(self.zuu.ap[pc_ * 8:(pc_ + 1) * 8].rearrange("c p t -> p c t"), self.zuu[pc_ * 8:(pc_ + 1) * 8].regs))
                for cc in range(4):
                    for th in range(2):
                        pb = self.pbank(cc * 2 + th, (512,))
                        for kk in range(8):
                            P.mm(pb, w[kk, cc * 128:(cc + 1) * 128], up[kk, th * 512:(th + 1) * 512],
                                 pc_ == 0 and kk == 0, pc_ == 7 and kk == 7)
            for cc in range(4):
                for th in range(2):
                    yo = yon()
                    P.act(yo, self.pbank(cc * 2 + th, (512,)), AF.Copy)
                    P.dma(self.zyo[cg * 4 + cc, :, th * 512:(th + 1) * 512], yo)
        self.release(m)
        self.postnorm_residual(3)


def _na_tile_table(rpb_l, first, last):
    specs = [(0, -2, 6), (1, -2, 5), (3, -2, 5), (6, -2, 5), (7, -3, 6)]
    u = np.arange(128) // 64
    kc = np.arange(128) % 64
    v = np.arange(128) // 64
    c = np.arange(128) % 64
    col_start = np.clip(c - 8, 0, 48)
    colok = (kc[:, None] >= col_start[None, :]) & (kc[:, None] < col_start[None, :] + 16)
    dc = kc[:, None] - c[None, :] + 15
    out = np.full((8, 128, 27, 128), NEG, np.float32)
    ti = 0
    for (m, d0, nt) in specs:
        for s in range(nt):
            delta = d0 + s
            if first:
                R0 = 0
            elif last:
                R0 = 48
            else:
                R0 = 16
            r = R0 + 2 * m + v
            kr = R0 + 2 * (m + delta) + u
            row_start = np.clip(r - 4, 0, 56)
            ok = (kr[:, None] >= row_start[None, :]) & (kr[:, None] < row_start[None, :] + 8)
            ok &= (kr[:, None] >= 0) & (kr[:, None] <= 63) & colok
            dr = kr[:, None] - r[None, :] + 7
            drc = np.clip(dr, 0, 14)
            dcc = np.clip(dc, 0, 30)
            g = rpb_l[:, drc, dcc]
            out[:, :, ti, :] = np.where(ok[None], g, np.float32(NEG))
            ti += 1
    assert ti == 27
    return out


def _to_fm(a):
    return np.ascontiguousarray(a.T.reshape(KC, 128, a.shape[0]).transpose(1, 0, 2))


def _from_fm(a):
    return np.ascontiguousarray(a.transpose(1, 0, 2).reshape(D, a.shape[2]).T)


class Host:
    def __init__(self, inp):
        f = np.float32
        self.inp = inp
        self.x = np.asarray(inp["x"], f)
        self.c = np.asarray(inp["c"], f)
        self.b_ada = np.ascontiguousarray(np.asarray(inp["b_ada"], f).reshape(DEPTH, 96, 128).transpose(0, 2, 1))
        gv = np.stack([inp["g_pre_mix"], inp["g_post_mix"], inp["g_pre_mlp"], inp["g_post_mlp"]], 1).astype(f)
        self.gvec = np.ascontiguousarray(gv.reshape(DEPTH, 4, KC, 128).transpose(0, 3, 1, 2))
        self.dec = np.ascontiguousarray(np.stack([inp["ret_decay_fwd"], inp["ret_decay_bwd"]], 1), f)
        self.scw = np.ascontiguousarray(np.asarray(inp["sc_conv"], f).reshape(DEPTH, 3, 8, 128).transpose(0, 3, 2, 1))
        self.cfw = np.ascontiguousarray(np.asarray(inp["cf_conv"], f).reshape(DEPTH, 31, 8, 128).transpose(0, 3, 2, 1))
        ln = np.stack([inp["cf_ln_g"], inp["cf_ln_b"]], 1).astype(f)
        self.cfln = np.ascontiguousarray(ln.reshape(DEPTH, 2, 8, 128).transpose(0, 3, 1, 2))
        i = np.arange(128, dtype=f)
        cst = np.zeros((128, 9, 128), f)
        cst[:, 0] = np.eye(128, dtype=f)
        cst[:, 1] = 1.0
        cst[:, 2] = np.abs(i[None, :] - i[:, None])
        cst[:, 3] = (i[None, :] >= i[:, None])
        cst[:, 4] = (i[:, None] > i[None, :])
        cst[:, 5] = i[None, :] + 1.0
        cst[:, 6] = 128.0 - i[None, :]
        self.cst = cst
        rpb = np.asarray(inp["na_rpb"], f)
        self.tabs = {}
        for kind in ((True, False), (False, False), (False, True)):
            self.tabs[kind] = [np.ascontiguousarray(_na_tile_table(rpb[l], *kind)) for l in range(DEPTH)]
        inv = (np.float32(10000.0) ** (-np.arange(64, dtype=f) / np.float32(64))).astype(f)
        self.rope = []
        self.pcst = []
        for j in range(NCORES):
            b, q = j // 4, j % 4
            off = q * T
            pos = np.arange(off, off + T).astype(f)
            ang = pos[None, :] * inv[:, None]
            cs = np.cos(ang).astype(f)
            sn = np.sin(ang).astype(f)
            rope = np.zeros((128, 2, T), f)
            rope[:64, 0] = cs
            rope[64:, 0] = cs
            rope[:64, 1] = -sn
            rope[64:, 1] = sn
            self.rope.append(rope)
            pc = np.zeros((128, 64), f)
            pc[:, 0] = 0.0 if q == 0 else 1.0
            pc[:, 1] = 0.0 if q == 3 else 1.0
            for s in range(8):
                sb, sq = s // 4, s % 4
                if sb == b and sq < q:
                    pc[:, 2 + s] = 1024.0 * (q - 1 - sq)
                    pc[:, 10 + s] = 1.0
                if sb == b and sq > q:
                    pc[:, 18 + s] = 1024.0 * (sq - q - 1)
                    pc[:, 26 + s] = 1.0
            p = np.arange(128, dtype=f)
            for cc in range(8):
                pc[:, 34 + cc] = 1023.0 - (cc * 128 + p)
                pc[:, 42 + cc] = cc * 128 + p
            self.pcst.append(pc)
        self.xT = [_to_fm(self.x[j // 4, (j % 4) * T:(j % 4 + 1) * T, :]) for j in range(NCORES)]

    def maps_M(self):
        cc = np.ascontiguousarray(self.c.reshape(2, KC, 128).transpose(2, 1, 0))
        wa = np.asarray(self.inp["w_ada"], np.float32)
        return [{"ccol2": cc, "w_ada_s": np.ascontiguousarray(wa[:, :, j * 1536:(j + 1) * 1536])} for j in range(NCORES)]

    def set_mod(self, res):
        mod = np.zeros((DEPTH, 2, 6 * D), np.float32)
        for j in range(NCORES):
            mp = np.asarray(res[j]["modp"])
            mod[:, :, j * 1536:(j + 1) * 1536] = mp.transpose(1, 3, 2, 0).reshape(DEPTH, 2, 1536)
        self.modc = np.ascontiguousarray(mod.reshape(DEPTH, 2, 96, 128).transpose(0, 1, 3, 2))

    def common(self, l, j):
        return {"xT": self.xT[j], "modc": self.modc[l, j // 4], "b_ada": self.b_ada[l], "gvec": self.gvec[l],
                "dec": self.dec[l], "cst": self.cst, "rope": self.rope[j], "pcst": self.pcst[j]}

    def maps_A(self, l):
        wkv = np.ascontiguousarray(np.asarray(self.inp["w_in"], np.float32)[l][:, 1024:3072])
        ms = []
        for j in range(NCORES):
            m = self.common(l, j)
            m["w_kv"] = wkv
            ms.append(m)
        return ms

    def maps_B(self, l, resA):
        inp = self.inp
        f = np.float32
        halos = [np.asarray(r["halo_out"]) for r in resA]
        Lall = np.ascontiguousarray(np.stack([np.asarray(r["Lout"], f) for r in resA], 0))
        w_in = np.ascontiguousarray(np.asarray(inp["w_in"], f)[l])
        w_bo = np.ascontiguousarray(np.stack([np.asarray(inp[k], f)[l] for k in ("w_ret_o", "w_na_o", "w_sc_o", "w_cf_o")], 0))
        w_o = np.ascontiguousarray(np.asarray(inp["w_o"], f)[l])
        w_ff1 = np.ascontiguousarray(np.asarray(inp["w_ff1"], f)[l])
        w_ff2 = np.ascontiguousarray(np.asarray(inp["w_ff2"], f)[l])
        ms = []
        for j in range(NCORES):
            q = j % 4
            m = self.common(l, j)
            halo = np.zeros_like(halos[j])
            if q > 0:
                halo[:, :, 0:256] = halos[j - 1][:, :, 256:512]
            if q < 3:
                halo[:, :, 256:512] = halos[j + 1][:, :, 0:256]
            m.update({"w_in": w_in, "nabias": self.tabs[(q == 0, q == 3)][l], "scw": self.scw[l], "cfw": self.cfw[l],
                      "cfln": self.cfln[l], "w_bo": w_bo, "w_o": w_o, "w_ff1": w_ff1, "w_ff2": w_ff2,
                      "halo": halo, "Lall": Lall})
            ms.append(m)
        return ms


def _ext_out(B, name, shape, dt=F32):
    t = B.nc.dram_tensor(name, list(shape), dt, kind="ExternalOutput")
    return t.ap()


def build_M():
    B = Builder("M")
    P = B.P
    cc = B.alloc([KC, 2], F32)
    P.dma(cc, B.ccol2)
    cact = B.alloc([KC, 2], BF16)
    P.act(cact, cc, AF.Silu)
    B.wb = [B.alloc([KC, 512], BF16) for _ in range(2)]
    B.wi = 0
    res = B.alloc([DEPTH, 12, 2], F32)
    out = _ext_out(B, "modp", [128, DEPTH, 12, 2])
    n = 0
    for l in range(DEPTH):
        for cg in range(3):
            w = B.load_w(B.w_ada_s[l][:, cg * 512:(cg + 1) * 512])
            for c4 in range(4):
                j = cg * 4 + c4
                pb = B.pbank(n % 8, (2,))
                n += 1
                for k in range(KC):
                    P.mm(pb, w[k, c4 * 128:(c4 + 1) * 128], cact[k], k == 0, k == KC - 1)
                P.copy("dve", res[l, j], pb)
    ov = V(out, [P.buf(("d", "modp", 0))])
    P.dma(ov, res)
    P.wait_all("sp", [ov])
    P.finalize()
    return B


def _common_front(B):
    B.setup()
    B.layer_tables()
    B.mod()
    B.norm_to_h(lambda k: B.der[0, k:k + 1], lambda k: B.modT[k:k + 1])


def build_A():
    B = Builder("A")
    P = B.P
    _common_front(B)
    B.scratch([("zk", [8, 128, T], BF16), ("zK", [8, 128, T], BF16), ("zv", [8, 128, T], BF16)])
    ho = _ext_out(B, "halo_out", [128, KC, 512], BF16)
    Lo = V(B.nc.dram_tensor("Lout", [2, 8, 128, 128], F32, kind="ExternalOutput").ap(), [])
    hv = V(ho[:, :, 0:256], [P.buf(("d", "halo_out", 0))])
    hv2 = V(ho[:, :, 256:512], [P.buf(("d", "halo_out", 1))])
    P.dma(hv, B.hT[0:KC, 0:256])
    P.dma(hv2, B.hT[0:KC, 768:1024])
    B.outs += [hv, hv2]
    for g2 in range(2):
        w = B.load_w(B.w_kv[:, g2 * 512:(g2 + 1) * 512])
        B.proj_rope(w, g2, B.zk, True)
    for g2 in range(2):
        w = B.load_w(B.w_kv[:, 1024 + g2 * 512:1024 + (g2 + 1) * 512])
        B.proj_tok(w, B.own_tok_srcs(), B.zv, g2 * 512)
    B.l_phase(Lo)
    P.wait_all("sp", B.outs)
    P.finalize()
    return B


def build_B(dbg=False, upto=99, dumps=(), only=None):
    B = Builder("B")
    P = B.P
    _common_front(B)
    B.hH = B.alloc([KC, 512], BF16)
    P.dma(B.hH, B.halo_in)
    B.scratch([
        ("zq", [8, 128, T], BF16), ("zk", [8, 128, T], BF16), ("zK", [8, 128, T], BF16), ("zv", [8, 128, T], BF16),
        ("zg", [8, 128, T], BF16), ("znq", [8, 128, T], BF16), ("znk", [8, 128, TE], BF16), ("znv", [12, 128, T], BF16),
        ("zsb", [8, 128, T], BF16), ("zp", [8, 128, 1056], BF16), ("zu", [8, 128, 1056], BF16),
        ("zgate", [64, 128, T], BF16), ("zy", [32, 128, T], BF16), ("zmg", [4, 128, 4, T], BF16),
        ("zyo", [16, 128, T], F32), ("zuu", [64, 128, T], BF16)])

    def wg(g):
        return B.load_w(B.w_in[:, g * 512:(g + 1) * 512])
    e0s = (0, 256, 768, 1280)
    ens = (256, 512, 512, 256)

    def st1():
        for g2 in range(2):
            B.proj_rope(wg(0 + g2), g2, B.zq, False, scale=128 ** -0.5)
        for g2 in range(2):
            B.proj_rope(wg(2 + g2), g2, B.zk, True)

    def st2():
        for g2 in range(2):
            B.proj_tok(wg(4 + g2), B.own_tok_srcs(), B.zv, g2 * 512)
        for g2 in range(2):
            B.proj_simple(wg(6 + g2), B.own_chunks(), AF.Silu,
                          lambda cc, ci, g2=g2: B.zg[g2 * 4 + cc, :, ci * 512:(ci + 1) * 512])
        for g2 in range(2):
            B.proj_simple(wg(8 + g2), B.own_chunks(), None,
                          lambda cc, ci, g2=g2: B.znq[g2 * 4 + cc, :, ci * 512:(ci + 1) * 512])

    def st3():
        for g2 in range(2):
            B.proj_simple(wg(10 + g2), B.ext_chunks(), None,
                          lambda cc, ci, g2=g2: B.znk[g2 * 4 + cc, :, e0s[ci]:e0s[ci] + ens[ci]])
        for g2 in range(2):
            B.proj_tok(wg(12 + g2), B.ext_tok_srcs(), B.znv, g2 * 512)
        for g2 in range(2):
            B.proj_simple(wg(14 + g2), B.own_chunks(), None,
                          lambda cc, ci, g2=g2: B.zsb[g2 * 4 + cc, :, ci * 512:(ci + 1) * 512])

    def st4():
        B.proj_pairs(8192, 9216, AF.Copy, B.zp)
        B.proj_pairs(11264, 10240, AF.Sigmoid, B.zu)

    def st5():
        for g in range(16):
            B.proj_simple(wg(24 + g), B.own_chunks(), AF.Sigmoid,
                          lambda cc, ci, g=g: B.zgate[g * 4 + cc, :, ci * 512:(ci + 1) * 512])

    def st6():
        B.release(B.hT_mark)
        for h in range(8):
            B.ret_head(h)
        B.release(B.base_mark)

    def st7():
        B.release(B.hT_mark)
        for h in range(8):
            B.na_head(h)
        B.release(B.base_mark)

    def st8():
        B.release(B.hT_mark)
        B.sc_branch()
        B.cf_branch()
        B.release(B.base_mark)

    def st9():
        B.release(B.hT_mark)
        B.merge()
        B.release(B.base_mark)

    def st10():
        B.dense_to_zyo(lambda cg: B.w_o[:, cg * 512:(cg + 1) * 512])
        B.postnorm_residual(2)

    def st11():
        B.mlp()
    stages = [st1, st2, st3, st4, st5, st6, st7, st8, st9, st10, st11]
    for i, st in enumerate(stages):
        if i + 1 > upto:
            break
        if only is not None and (i + 1) not in only:
            continue
        st()
        if (i + 1) == 9 and dbg:
            o = _ext_out(B, "dbg_mg", [128, KC, T], BF16)
            ov = V(o, [P.buf(("d", "dbg_mg", 0))])
            P.dma(ov, B.hT)
            B.outs.append(ov)
        if (i + 1) == 10 and dbg:
            o = _ext_out(B, "dbg_x1", [128, KC, T], F32)
            for k in range(KC):
                ov = V(o[:, k, :], [P.buf(("d", "dbg_x1", k))])
                P.dma(ov, B.xv(k))
                B.outs.append(ov)
    for nm in dumps:
        d = getattr(B, nm)
        o = _ext_out(B, "dbg_" + nm, list(d.shape), d.dt)
        ov = V(o, [P.buf(("d", "dbg_" + nm, 0))])
        P.dma(ov, d[0:d.shape[0]])
        B.outs.append(ov)
    xo = _ext_out(B, "xT_out", [128, KC, T])
    for k in range(KC):
        ov = V(xo[:, k, :], [P.buf(("d", "xT_out", k))])
        P.dma(ov, B.xv(k))
        B.outs.append(ov)
    P.wait_all("sp", B.outs)
    P.finalize()
    return B


_PROGS = {}


def _prog(kind, **kw):
    if kind not in _PROGS:
        _PROGS[kind] = {"M": build_M, "A": build_A, "B": build_B}[kind](**kw)
    return _PROGS[kind]


def _launch(B, maps):
    names = B.in_names
    res = run_bass_kernel_spmd(B.nc, [{k: m[k] for k in names} for m in maps], core_ids=list(range(NCORES)))
    return res.results


def kernel(**inputs):
    H = Host(inputs)
    H.set_mod(_launch(_prog("M"), H.maps_M()))
    for l in range(DEPTH):
        resA = _launch(_prog("A"), H.maps_A(l))
        resB = _launch(_prog("B"), H.maps_B(l, resA))
        H.xT = [np.asarray(r["xT_out"], np.float32) for r in resB]
    out = np.zeros((2, 4 * T, D), np.float32)
    for j in range(NCORES):
        out[j // 4, (j % 4) * T:(j % 4 + 1) * T, :] = _from_fm(H.xT[j])
    return out
```

```python
import numpy as np
import concourse.bass as bass
import concourse.mybir as mybir
from concourse.bass_utils import run_bass_kernel_spmd

F32 = mybir.dt.float32
BF16 = mybir.dt.bfloat16
U8 = mybir.dt.uint8
AF = mybir.ActivationFunctionType
ALU = mybir.AluOpType

NCORES = 8
DEPTH = 4
D = 2048
KC = 16
T = 1024
HAL = 256
TE = T + 2 * HAL
DIN = 20480
EPS = 1e-6
REG = 1024
ESZ = {F32: 4, BF16: 2, U8: 1}
NA_SCALE = 128 ** -0.5
NEG = -30000.0


class Buf:
    __slots__ = ("name", "w", "r")

    def __init__(self, name):
        self.name = name
        self.w = {}
        self.r = {}


class V:
    __slots__ = ("ap", "regs")

    def __init__(self, ap, regs):
        self.ap = ap
        self.regs = regs


def _rng(shape, key):
    if not isinstance(key, tuple):
        key = (key,)
    key = key + (slice(None),) * (len(shape) - len(key))
    strides = []
    s = 1
    for n in reversed(shape):
        strides.append(s)
        s *= n
    strides = strides[::-1]
    lo = 0
    hi = 0
    for k, n, st in zip(key, shape, strides):
        if isinstance(k, int):
            a, b = k, k + 1
        else:
            a = 0 if k.start is None else k.start
            b = n if k.stop is None else k.stop
        assert 0 <= a < b <= n, (shape, key)
        lo += a * st
        hi += (b - 1) * st
    return lo, hi + 1, key


class Tl:
    def __init__(self, prog, space, off, shape, dt, parts=128):
        self.prog, self.space, self.off, self.shape, self.dt = prog, space, off, tuple(shape), dt
        self.esz = ESZ[dt]
        n = int(np.prod(shape)) * self.esz
        self.nbytes = n
        base = prog.arena if space == "sb" else prog.psum
        gran = REG if space == "sb" else 2048
        self.gran = gran
        if space == "sb":
            ap = base[0:parts, off:off + n].bitcast(dt)
        else:
            ap = base[0:parts, off // 4:(off + n) // 4]
            if dt != F32:
                ap = ap.bitcast(dt)
        if len(shape) == 2:
            ap = ap.rearrange("p (a b) -> p a b", b=shape[1])
        elif len(shape) == 3:
            ap = ap.rearrange("p (a b c) -> p a b c", b=shape[1], c=shape[2])
        self.ap = ap

    def _regs(self, lo_e, hi_e):
        lo = self.off + lo_e * self.esz
        hi = self.off + hi_e * self.esz
        return [self.prog.buf((self.space, r)) for r in range(lo // self.gran, (hi - 1) // self.gran + 1)]

    @property
    def regs(self):
        return self._regs(0, int(np.prod(self.shape)))

    def __getitem__(self, key):
        lo, hi, key = _rng(self.shape, key)
        return V(self.ap[(slice(None),) + key], self._regs(lo, hi))

    def pv(self, p0, p1, key=()):
        lo, hi, key = _rng(self.shape, key)
        return V(self.ap[(slice(p0, p1),) + key], self._regs(lo, hi))


class Dr:
    def __init__(self, prog, name, shape, dt, kind="Internal"):
        self.prog, self.name, self.shape = prog, name, tuple(shape)
        self.t = prog.nc.dram_tensor(name, list(shape), dt, kind=kind)
        self.dt = dt
        self.ap = self.t.ap()

    def __getitem__(self, key):
        if not isinstance(key, tuple):
            key = (key,)
        k0 = key[0]
        if isinstance(k0, int):
            idx = range(k0, k0 + 1)
        else:
            idx = range(k0.start or 0, self.shape[0] if k0.stop is None else k0.stop)
        return V(self.ap[key], [self.prog.buf(("d", self.name, i)) for i in idx])


def _ap(x):
    return x.ap if isinstance(x, (V, Tl)) else x


def _regs(x):
    return x.regs if isinstance(x, (V, Tl)) else []


class Prog:
    def __init__(self):
        self.nc = bass.Bass("TRN2", target_bir_lowering=False)
        self.ops = []
        self.labels = []
        self.label = None
        self.scoped = False
        self._bufs = {}
        self.arena = None
        self.psum = None

    def buf(self, name):
        b = self._bufs.get(name)
        if b is None:
            b = self._bufs[name] = Buf(name)
        return b

    def _op(self, eng, fn, reads, writes, dma=False):
        r = []
        for x in reads:
            r += _regs(x)
        w = []
        for x in writes:
            w += _regs(x)
        self.ops.append((eng, fn, r, w, dma))
        self.labels.append(self.label)

    def mm(self, out, lhsT, rhs, start, stop):
        o, l, r = _ap(out), _ap(lhsT), _ap(rhs)
        self._op("pe", lambda e: e.matmul(o, lhsT=l, rhs=r, start=start, stop=stop), [lhsT, rhs], [out])

    def tr(self, out, in_, ident):
        o, i, d = _ap(out), _ap(in_), _ap(ident)
        self._op("pe", lambda e: e.transpose(o, i, d), [in_, ident], [out])

    def act(self, out, in_, func, bias=None, scale=None, eng="act"):
        o, i = _ap(out), _ap(in_)
        kw = {}
        rd = [in_]
        if bias is not None:
            kw["bias"] = _ap(bias)
            rd.append(bias)
        if scale is not None:
            kw["scale"] = _ap(scale)
            rd.append(scale)
        self._op(eng, lambda e: e.activation(out=o, in_=i, func=func, **kw), rd, [out])

    def tt(self, eng, out, in0, in1, op):
        o, a, b = _ap(out), _ap(in0), _ap(in1)
        self._op(eng, lambda e: e.tensor_tensor(out=o, in0=a, in1=b, op=op), [in0, in1], [out])

    def ts(self, eng, out, in0, s1, s2, op0, op1=None):
        o, a = _ap(out), _ap(in0)
        a1, a2 = _ap(s1), _ap(s2)
        kw = {} if op1 is None else {"op1": op1}
        self._op(eng, lambda e: e.tensor_scalar(out=o, in0=a, scalar1=a1, scalar2=a2, op0=op0, **kw),
                 [in0, s1, s2], [out])

    def stt(self, eng, out, in0, scalar, in1, op0, op1):
        o, a, b, s = _ap(out), _ap(in0), _ap(in1), _ap(scalar)
        self._op(eng, lambda e: e.scalar_tensor_tensor(out=o, in0=a, scalar=s, in1=b, op0=op0, op1=op1),
                 [in0, in1, scalar], [out])

    def copy(self, eng, out, in_):
        o, i = _ap(out), _ap(in_)
        self._op(eng, lambda e: e.tensor_copy(out=o, in_=i), [in_], [out])

    def recip(self, out, in_):
        o, i = _ap(out), _ap(in_)
        self._op("dve", lambda e: e.reciprocal(out=o, in_=i), [in_], [out])

    def memset(self, eng, out, val):
        o = _ap(out)
        self._op(eng, lambda e: e.memset(o, val), [], [out])

    def dma(self, out, in_, q="sp"):
        o, i = _ap(out), _ap(in_)
        self._op(q, lambda e: e.dma_start(out=o, in_=i), [in_], [out], dma=True)

    def wait_all(self, eng, views):
        self._op(eng, None, views, [])

    def finalize(self):
        nc = self.nc
        ops = self.ops
        n = len(ops)
        deps = [None] * n
        sig = [False] * n
        KRING = 8
        dma_hist = {"sp": [], "pool": [], "act": []}
        for i, (eng, fn, r, w, dma) in enumerate(ops):
            d = set()
            for b in r:
                d.update(b.w.values())
                if b.name[0] == "ps":
                    d.update(v for k_, v in b.r.items() if k_ != eng)
            for b in w:
                d.update(b.w.values())
                d.update(b.r.values())
            d.discard(i)
            key = ("d", i) if dma else eng
            for b in r:
                b.r[key] = i
            for b in w:
                b.w = {key: i}
                b.r = {}
            if dma:
                h = dma_hist[eng]
                if len(h) >= KRING:
                    d.add(h[-KRING])
                h.append(i)
            dd = []
            for j in d:
                ej, _, _, _, dj = ops[j]
                if (not dj) and ej == eng and eng == "pe":
                    continue
                dd.append(j)
                sig[j] = True
            deps[i] = dd
        engs = {"pe": nc.tensor, "act": nc.scalar, "dve": nc.vector, "pool": nc.gpsimd, "sp": nc.sync}
        csem = {e: nc.alloc_semaphore(name="c_" + e) for e in ("pe", "act", "dve", "pool")}
        rings = {q: [nc.alloc_semaphore(name=f"d_{q}{k}") for k in range(KRING)] for q in ("sp", "pool")}
        cnt = {e: 0 for e in csem}
        dcnt = {q: 0 for q in rings}
        sv = [None] * n
        seen = {e: {} for e in engs}
        nwait = 0
        cur = None
        for i, (eng, fn, r, w, dma) in enumerate(ops):
            if self.scoped and self.labels[i] != cur:
                if cur is not None:
                    nc.pop_named_scope(cur)
                cur = self.labels[i]
                if cur is not None:
                    nc.push_named_scope(cur)
            e = engs[eng]
            need = {}
            for j in deps[i]:
                s, v = sv[j]
                if need.get(s.num if hasattr(s, "num") else id(s), (None, -1))[1] < v:
                    need[s.num if hasattr(s, "num") else id(s)] = (s, v)
            for k, (s, v) in need.items():
                if seen[eng].get(k, -1) >= v:
                    continue
                seen[eng][k] = v
                e.wait_ge(s, v)
                nwait += 1
            if fn is None:
                continue
            ins = fn(e)
            if dma:
                q = dcnt[eng]
                dcnt[eng] += 1
                s = rings[eng][q % KRING]
                v = 16 * (q // KRING + 1)
                ins.then_inc(s, 16)
                sv[i] = (s, v)
            elif sig[i]:
                cnt[eng] += 1
                ins.then_inc(csem[eng], 1)
                sv[i] = (csem[eng], cnt[eng])
        if self.scoped and cur is not None:
            nc.pop_named_scope(cur)
        print(f"[prog] ops={n} waits={nwait} sem_counts={cnt} dmas={dcnt}", flush=True)


class Builder:
    def __init__(self, kind):
        self.kind = kind
        self.P = P = Prog()
        nc = P.nc
        self.nc = nc
        self.outs = []
        self.in_names = []

        specs = {}
        if kind == "M":
            specs.update(ccol2=("ccol2", [128, KC, 2], F32), w_ada_s=("w_ada_s", [DEPTH, D, 1536], F32))
        else:
            specs.update(x_in=("xT", [128, KC, T], F32), modc=("modc", [128, 96], F32), b_ada=("b_ada", [128, 96], F32),
                         gvec=("gvec", [128, 4, KC], F32), dec=("dec", [2, 8], F32), cst=("cst", [128, 9, 128], F32),
                         rope=("rope", [128, 2, T], F32), pcst=("pcst", [128, 64], F32),
                         w_kv=("w_kv", [D, 2048], F32), w_in=("w_in", [D, DIN], F32),
                         nabias=("nabias", [8, 128, 27, 128], F32), scw=("scw", [128, 8, 3], F32),
                         cfw=("cfw", [128, 8, 31], F32), cfln=("cfln", [128, 2, 8], F32),
                         w_bo=("w_bo", [4, 1024, D], F32), w_o=("w_o", [D, D], F32), w_ff1=("w_ff1", [D, 4 * D], F32),
                         w_ff2=("w_ff2", [4 * D, D], F32), halo_in=("halo", [128, KC, 512], BF16),
                         L_in=("Lall", [8, 2, 8, 128, 128], F32))
        self._specs = specs
        if kind != "M":
            self.xT = nc.alloc_sbuf_tensor("xT_sb", [128, KC, T], F32)
            self.xbuf = [[P.buf(("x", k, h)) for h in range(2)] for k in range(KC)]
        free = nc.sbuf_bytes_remaining
        self.ASZ = (free - 2048) // 1024 * 1024
        P.arena = nc.alloc_sbuf_tensor("arena", [128, self.ASZ], U8)
        P.psum = nc.alloc_psum_tensor("psum", [128, 4096], F32)
        self._off = 0

    def __getattr__(self, name):
        specs = self.__dict__.get("_specs", {})
        if name in specs:
            nm, shape, dt = specs[name]
            ap = self.nc.dram_tensor(nm, list(shape), dt, kind="ExternalInput").ap()
            self.in_names.append(nm)
            self.__dict__[name] = ap
            return ap
        raise AttributeError(name)

    def alloc(self, shape, dt, parts=128):
        n = int(np.prod(shape)) * ESZ[dt]
        n_al = (n + REG - 1) // REG * REG
        off = self._off
        assert off + n_al <= self.ASZ, ("arena overflow", off, n_al, self.ASZ)
        self._off += n_al
        return Tl(self.P, "sb", off, shape, dt, parts)

    def mark(self):
        return self._off

    def release(self, m):
        self._off = m

    def pbank(self, b, shape=(512,), dt=F32, nb=1):
        return Tl(self.P, "ps", b * 2048, shape, dt)

    def xv(self, k, t0=0, n=T):
        regs = []
        if t0 < 512:
            regs.append(self.xbuf[k][0])
        if t0 + n > 512:
            regs.append(self.xbuf[k][1])
        return V(self.xT[:, k, t0:t0 + n], regs)

    def dump(self, name, view, shape, dt=F32):
        o = Dr(self.P, "dbg_" + name, shape, dt, kind="ExternalOutput")
        self.P.dma(V(o.ap, [self.P.buf(("d", "dbg_" + name, 0))]), view)
        self.outs.append(V(o.ap, [self.P.buf(("d", "dbg_" + name, 0))]))

    def setup(self):
        P = self.P
        self.C = self.alloc([9, 128], F32)
        P.dma(self.C, self.cst)
        self.ropeT = self.alloc([2, T], F32)
        P.dma(self.ropeT, self.rope)
        self.pc = self.alloc([64], F32)
        P.dma(self.pc, self.pcst)
        self.identb = self.alloc([128], BF16)
        self.onesb = self.alloc([128], BF16)
        P.copy("dve", self.identb, self.C[0])
        P.copy("dve", self.onesb, self.C[1])
        self.modT = self.alloc([96], F32)
        self.gv = self.alloc([4, KC], F32)
        self.der = self.alloc([6, KC], F32)
        self.lg = self.alloc([16], F32)
        self.gC = self.alloc([16], F32)
        self.DT = self.alloc([8, 128], F32)
        self.QD = self.alloc([2, 8, 128], F32)
        self.KDD = self.alloc([2, 8, 8], F32)
        self.coef = self.alloc([2, 8, 8], F32)
        self.wb = [self.alloc([KC, 512], BF16) for _ in range(2)]
        self.wi = 0
        self.wh = [Tl(self.P, "sb", self.wb[s_].off + h_ * 8 * 512 * 2, [8, 512], BF16) for s_ in range(2) for h_ in range(2)]
        self.whi = 0
        self._pr = 0
        self.hT_mark = self.mark()
        self.hT = self.alloc([KC, T], BF16)
        self.base_mark = self.mark()
        for k in range(KC):
            P.dma(self.xv(k), self.x_in[:, k, :])

    def next_w(self):
        w = self.wb[self.wi % 2]
        self.wi += 1
        return w

    def load_w(self, src, nk=KC, cols=512, c0=0, w=None):
        if w is None and nk == 8:
            w = self.wh[self.whi % 4]
            self.whi += 1
        if w is None:
            w = self.next_w()
        self.P.dma(w[0:nk, c0:c0 + cols], src.rearrange("(k p) c -> p k c", p=128), q="pool")
        return w

    def layer_tables(self):
        P = self.P
        m = self.mark()
        lgt = self.alloc([16], F32)
        P.dma(lgt, self.dec.rearrange("a h -> (a h)").partition_broadcast(128))
        e = self.alloc([16], F32)
        t = self.alloc([16], F32)
        P.act(e, lgt, AF.Exp, scale=-1.0)
        P.ts("dve", t, e, -0.2, 0.25, ALU.mult, ALU.add)
        for cst_ in (1.0 / 3, 0.5, 1.0):
            P.tt("dve", t, t, e, ALU.mult)
            P.ts("dve", t, t, -1.0, cst_, ALU.mult, ALU.add)
        P.tt("dve", t, t, e, ALU.mult)
        P.ts("dve", self.lg, t, -1.0, None, ALU.mult)
        P.act(self.gC, self.lg, AF.Exp, scale=128.0)
        t1 = self.alloc([128], F32)
        t2 = self.alloc([128], F32)
        for h in range(8):
            lf = self.lg[h:h + 1]
            lb = self.lg[8 + h:9 + h]
            P.act(t1, self.C[2], AF.Exp, scale=lf)
            P.act(t2, self.C[2], AF.Exp, scale=lb)
            P.tt("dve", t1, t1, self.C[3], ALU.mult)
            P.tt("dve", t2, t2, self.C[4], ALU.mult)
            P.tt("dve", self.DT[h], t1, t2, ALU.add)
            P.act(self.QD[0, h], self.C[5], AF.Exp, scale=lf)
            P.act(self.QD[1, h], self.C[6], AF.Exp, scale=lb)
            P.act(self.KDD[0, h], self.pc[34:42], AF.Exp, scale=lf)
            P.act(self.KDD[1, h], self.pc[42:50], AF.Exp, scale=lb)
            P.act(self.coef[0, h], self.pc[2:10], AF.Exp, scale=lf)
            P.act(self.coef[1, h], self.pc[18:26], AF.Exp, scale=lb)
            P.tt("dve", self.coef[0, h], self.coef[0, h], self.pc[10:18], ALU.mult)
            P.tt("dve", self.coef[1, h], self.coef[1, h], self.pc[26:34], ALU.mult)
        self.release(m)

    def mod(self):
        P = self.P
        m = self.mark()
        bc = self.alloc([96], F32)
        P.dma(bc, self.b_ada)
        P.dma(self.gv, self.gvec)
        mc = self.alloc([96], F32)
        P.dma(mc, self.modc)
        P.tt("dve", self.modT, mc, bc, ALU.add)
        P.ts("dve", self.der[4], self.modT[16:32], 1.0, None, ALU.add)
        P.tt("dve", self.der[0], self.der[4], self.gv[0], ALU.mult)
        P.ts("dve", self.der[5], self.modT[64:80], 1.0, None, ALU.add)
        P.tt("dve", self.der[1], self.der[5], self.gv[2], ALU.mult)
        P.tt("dve", self.der[2], self.modT[32:48], self.gv[1], ALU.mult)
        P.tt("dve", self.der[3], self.modT[80:96], self.gv[3], ALU.mult)
        self.release(m)

    def rstd_from_sq(self, src_fn, nk, inv_n, out_rstd, loader=None, ring=None):
        P = self.P
        m = self.mark()
        sq = [self.alloc([T], BF16) for _ in range(3)]
        pb = self.pbank(4, (2, 512))
        for k in range(nk):
            s = sq[k % 3]
            if loader is not None:
                t = ring()
                loader(k, t)
                P.act(s, t, AF.Square)
            else:
                P.act(s, src_fn(k), AF.Square)
            for th in range(2):
                P.mm(pb[th], self.onesb, s[th * 512:(th + 1) * 512], k == 0, k == nk - 1)
        P.ts("dve", out_rstd, pb, inv_n, EPS, ALU.mult, ALU.add)
        P.act(out_rstd, out_rstd, AF.Ln)
        P.act(out_rstd, out_rstd, AF.Exp, scale=-0.5)
        self.release(m)

    def norm_to_h(self, s_fn, sh_fn):
        P = self.P
        m = self.mark()
        rstd = self.alloc([T], F32)
        self.rstd_from_sq(lambda k: self.xv(k), KC, 1.0 / D, rstd)
        xr = [self.alloc([T], F32) for _ in range(2)]
        for k in range(KC):
            t = xr[k % 2]
            P.tt("dve", t, self.xv(k), rstd, ALU.mult)
            P.act(self.hT[k], t, AF.Identity, bias=sh_fn(k), scale=s_fn(k))
        self.release(m)


    def scratch(self, names):
        for name, shape, dt in names:
            setattr(self, name, Dr(self.P, name, shape, dt))

    def proj(self, w, nk, ccs, rhs_chunks, epi, banks=(0, 1, 2, 3)):
        P = self.P
        for cc in ccs:
            for ci, (rhs_fn, n) in enumerate(rhs_chunks):
                b = banks[self._pr % len(banks)]
                self._pr += 1
                pb = self.pbank(b, (512,))
                for k in range(nk):
                    P.mm(pb[0:n], w[k, cc * 128:(cc + 1) * 128], rhs_fn(k), k == 0, k == nk - 1)
                epi(cc, ci, pb[0:n], n)

    def own_chunks(self):
        return [((lambda k, t0=t0: self.hT[k, t0:t0 + 512]), 512) for t0 in (0, 512)]

    def ext_chunks(self):
        return [((lambda k: self.hH[k, 0:256]), 256),
                ((lambda k: self.hT[k, 0:512]), 512),
                ((lambda k: self.hT[k, 512:1024]), 512),
                ((lambda k: self.hH[k, 256:512]), 256)]

    def conv_chunks(self):
        return [((lambda k: self.hH[k, 240:256]), 16),
                ((lambda k: self.hT[k, 0:512]), 512),
                ((lambda k: self.hT[k, 512:1024]), 512),
                ((lambda k: self.hH[k, 256:272]), 16)]

    def ring(self, name, n, shape, dt):
        tiles = [self.alloc(shape, dt) for _ in range(n)]
        st = {"i": 0}

        def nxt():
            t = tiles[st["i"] % n]
            st["i"] += 1
            return t
        return nxt

    @staticmethod
    def vp(v, p0, p1):
        return V(v.ap[p0:p1], v.regs)

    def proj_rope(self, w, g2, dst, with_tok, scale=None):
        P = self.P
        m = self.mark()
        t1n = self.ring("t1", 2, [512], F32)
        t2n = self.ring("t2", 2, [512], F32)
        obn = self.ring("ob", 3, [512], BF16)
        ktn = self.ring("kt", 2, [4, 128], BF16)
        pend = []

        def flush():
            while pend:
                ob, ci, h = pend.pop(0)
                pt = self.pbank(4 + (self._pr % 2), (4, 128), BF16)
                for i in range(4):
                    P.tr(pt[i], ob[i * 128:(i + 1) * 128], self.identb)
                kt = ktn()
                P.act(kt, pt, AF.Copy)
                c0 = ci * 4
                P.dma(V(self.zK.ap[c0:c0 + 4, :, h * 128:(h + 1) * 128].rearrange("c j d -> j c d"),
                        self.zK[c0:c0 + 4].regs), kt)

        def epi(cc, ci, pb, n):
            h = g2 * 4 + cc
            t0 = ci * 512
            t1, t2, ob = t1n(), t2n(), obn()
            P.tt("dve", t1, pb, self.ropeT[0, t0:t0 + 512], ALU.mult)
            P.tt("dve", t2.pv(0, 64), self.vp(pb, 64, 128), self.ropeT.pv(0, 64, (1, slice(t0, t0 + 512))), ALU.mult)
            P.tt("dve", t2.pv(64, 128), self.vp(pb, 0, 64), self.ropeT.pv(64, 128, (1, slice(t0, t0 + 512))), ALU.mult)
            if scale is None:
                P.tt("dve", ob, t1, t2, ALU.add)
            else:
                P.tt("dve", t1, t1, t2, ALU.add)
                P.act(ob, t1, AF.Copy, scale=scale)
            P.dma(dst[h, :, t0:t0 + 512], ob)
            if with_tok:
                flush()
                pend.append((ob, ci, h))
        self.proj(w, KC, range(4), self.own_chunks(), epi)
        flush()
        self.release(m)

    def proj_tok(self, w, srcs, dst, c0col):
        P = self.P
        m = self.mark()
        obn = self.ring("obt", 2, [512], BF16)
        for e, src in enumerate(srcs):
            b = self._pr % 4
            self._pr += 1
            pb = self.pbank(b, (512,))
            for k in range(KC):
                P.mm(pb, src(k), w[k, 0:512], k == 0, k == KC - 1)
            ob = obn()
            P.act(ob, pb, AF.Copy)
            P.dma(dst[e, :, c0col:c0col + 512], ob)
        self.release(m)

    def own_tok_srcs(self):
        return [(lambda k, c=c: self.hT[k, c * 128:(c + 1) * 128]) for c in range(8)]

    def ext_tok_srcs(self):
        r = []
        for e in range(12):
            if e < 2:
                r.append(lambda k, e=e: self.hH[k, e * 128:(e + 1) * 128])
            elif e < 10:
                r.append(lambda k, e=e: self.hT[k, (e - 2) * 128:(e - 1) * 128])
            else:
                r.append(lambda k, e=e: self.hH[k, 256 + (e - 10) * 128:256 + (e - 9) * 128])
        return r

    def proj_simple(self, w, chunks, func, dst_fn, eng_alt=True):
        P = self.P
        m = self.mark()
        obn = self.ring("obs", 3, [512], BF16)

        def epi(cc, ci, pb, n):
            ob = obn()
            if func is None and (self._pr % 2 == 0):
                P.copy("dve", ob[0:n], pb)
            else:
                P.act(ob[0:n], pb, AF.Copy if func is None else func)
            P.dma(dst_fn(cc, ci), ob[0:n])
        self.proj(w, KC, range(4), chunks, epi)
        self.release(m)

    def load_tok(self, t, src, h, nchunks):
        self.P.dma(t, V(src.ap[0:nchunks, :, h * 128:(h + 1) * 128].rearrange("c j d -> j c d"),
                        src[0:nchunks].regs))

    def l_phase(self, Lout):
        P = self.P
        m = self.mark()
        ktn = self.ring("lk", 2, [8, 128], BF16)
        vtn = self.ring("lv", 2, [8, 128], BF16)
        kdn = self.ring("lkd", 2, [8, 128], BF16)
        lsn = self.ring("ls", 2, [128], F32)
        n = 0
        for h in range(8):
            kt, vt = ktn(), vtn()
            self.load_tok(kt, self.zK, h, 8)
            self.load_tok(vt, self.zv, h, 8)
            for dr in range(2):
                kd = kdn()
                for c in range(8):
                    sc = self.KDD[dr, h, c:c + 1]
                    if c % 2 == 0:
                        P.ts("dve", kd[c], kt[c], sc, None, ALU.mult)
                    else:
                        P.act(kd[c], kt[c], AF.Copy, scale=sc)
                pb = self.pbank(n % 4, (128,))
                n += 1
                for c in range(8):
                    P.mm(pb, kd[c], vt[c], c == 0, c == 7)
                ls = lsn()
                P.act(ls, pb, AF.Copy)
                P.dma(V(Lout.ap[dr, h], [P.buf(("d", "Lout", dr * 8 + h))]), ls)
                self.outs.append(V(Lout.ap[dr, h], [P.buf(("d", "Lout", dr * 8 + h))]))
        self.release(m)

    def rstd_chain(self, out, src):
        P = self.P
        P.ts("dve", out, src, EPS, None, ALU.add)
        P.act(out, out, AF.Ln)
        P.act(out, out, AF.Exp, scale=-0.5)

    def ret_head(self, h):
        import os
        CUT = int(os.environ.get('RCUT', '99'))
        P = self.P
        m = self.mark()
        A = self.alloc
        qT, kT, sg = A([8, 128], BF16), A([8, 128], BF16), A([8, 128], BF16)
        kt, vt = A([8, 128], BF16), A([8, 128], BF16)
        P.dma(qT, self.zq[h])
        P.dma(kT, self.zk[h])
        P.dma(sg, self.zg[h])
        self.load_tok(kt, self.zK, h, 8)
        self.load_tok(vt, self.zv, h, 8)
        La = A([8, 2, 128], F32)
        for s_ in range(8):
            P.dma(La[s_], self.L_in[s_, :, h].rearrange("r d v -> d r v"))
        if CUT < 1:
            self.release(m); return
        Sf, Sb = A([8, 128], F32), A([8, 128], F32)
        for dr, S, c0 in ((0, Sf, 0), (1, Sb, 7)):
            P.ts("dve", S[c0], La[0, dr], self.coef[dr, h, 0:1], None, ALU.mult)
            for s_ in range(1, 8):
                P.stt("dve", S[c0], La[s_, dr], self.coef[dr, h, s_:s_ + 1], S[c0], ALU.mult, ALU.add)
        if CUT < 2:
            self.release(m); return
        kdf, kdb = A([8, 128], BF16), A([8, 128], BF16)
        P.ts("dve", kdf, kt, self.KDD[0, h, 7:8], None, ALU.mult)
        P.act(kdb, kt, AF.Copy, scale=self.KDD[1, h, 0:1])
        Uf = self.pbank(0, (8, 128))
        Ub = self.pbank(2, (8, 128))
        for c in range(8):
            P.mm(Uf[c], kdf[c], vt[c], True, True)
        for c in range(8):
            P.mm(Ub[c], kdb[c], vt[c], True, True)
        for c in range(7):
            P.stt("dve", Sf[c + 1], Sf[c], self.gC[h:h + 1], Uf[c], ALU.mult, ALU.add)
        for c in range(7, 0, -1):
            P.stt("pool" if False else "dve", Sb[c - 1], Sb[c], self.gC[8 + h:9 + h], Ub[c], ALU.mult, ALU.add)
        if CUT < 3:
            self.release(m); return
        Sfb, Sbb = A([8, 128], BF16), A([8, 128], BF16)
        P.act(Sfb, Sf, AF.Copy)
        P.act(Sbb, Sb, AF.Copy)
        qdf, qdb = A([8, 128], BF16), A([8, 128], BF16)
        for dr, qd in ((0, qdf), (1, qdb)):
            qv = self.QD[dr, h]
            P.tt("dve", qd, qT,
                 V(qv.ap.unsqueeze(1).to_broadcast([128, 8, 128]), qv.regs), ALU.mult)
        if CUT < 4:
            self.release(m); return
        St = self.pbank(4, (8, 128))
        for c in range(8):
            P.mm(St[c], kT[c], qT[c], True, True)
        PT = A([8, 128], BF16)
        dv = self.DT[h]
        P.tt("dve", PT, St, V(dv.ap.unsqueeze(1).to_broadcast([128, 8, 128]), dv.regs), ALU.mult)
        if CUT < 5:
            self.release(m); return
        O = self.pbank(6, (8, 128))
        for c in range(8):
            P.mm(O[c], vt[c], PT[c], True, False)
            P.mm(O[c], Sfb[c], qdf[c], False, False)
            P.mm(O[c], Sbb[c], qdb[c], False, True)
        if CUT < 6:
            self.release(m); return
        y = A([1024], F32)
        ybf, ysq = A([1024], BF16), A([1024], BF16)
        Of = self.pbank(6, (1024,))
        for th in range(2):
            hs = slice(th * 512, (th + 1) * 512)
            P.act(y[hs], Of[hs], AF.Copy)
            P.act(ysq[hs], Of[hs], AF.Square)
            P.copy("dve", ybf[hs], Of[hs])
        mp = self.pbank(0, (1024,))
        sp = self.pbank(2, (1024,))
        for th in range(2):
            P.mm(mp[th * 512:(th + 1) * 512], self.onesb, ybf[th * 512:(th + 1) * 512], True, True)
            P.mm(sp[th * 512:(th + 1) * 512], self.onesb, ysq[th * 512:(th + 1) * 512], True, True)
        if CUT < 7:
            self.release(m); return
        mean, t2 = A([1024], F32), A([1024], F32)
        P.ts("dve", mean, mp, 1.0 / 128, None, ALU.mult)
        P.tt("pool", t2, mean, mean, ALU.mult)
        P.stt("dve", t2, sp, 1.0 / 128, t2, ALU.mult, ALU.subtract)
        self.rstd_chain(t2, t2)
        P.tt("dve", y, y, mean, ALU.subtract)
        P.tt("pool", y, y, t2, ALU.mult)
        yo = A([1024], BF16)
        P.tt("dve", yo, y, V(sg.ap.rearrange("p c i -> p (c i)"), sg.regs), ALU.mult)
        P.dma(self.zy[h], yo)
        self.release(m)

    def na_head(self, h):
        P = self.P
        m = self.mark()
        A = self.alloc
        qT, kT, vt = A([8, 128], BF16), A([12, 128], BF16), A([12, 128], BF16)
        bias = A([27, 128], F32)
        P.dma(qT, self.znq[h])
        P.dma(kT, self.znk[h])
        self.load_tok(vt, self.znv, h, 12)
        P.dma(bias, self.nabias[h])
        tmpn = self.ring("natmp", 2, [6, 128], F32)
        ptn = self.ring("napt", 2, [6, 128], BF16)
        O = self.pbank(4, (8, 128))
        Dn = self.pbank(6, (8, 128))
        specs = [(-2, 6, 0), (-2, 5, 6), (-2, 5, 11), (-2, 5, 11), (-2, 5, 11), (-2, 5, 11), (-2, 5, 16), (-3, 6, 21)]
        for mq in range(8):
            d0, nt, ti0 = specs[mq]
            St = self.pbank(0 if mq % 2 == 0 else 2, (6, 128))
            for s_ in range(nt):
                e = mq + 2 + d0 + s_
                P.mm(St[s_], kT[e], qT[mq], True, True)
            tmp, PT = tmpn(), ptn()
            P.stt("dve", tmp[0:nt], St[0:nt], NA_SCALE, bias[ti0:ti0 + nt], ALU.mult, ALU.add)
            P.act(PT[0:nt], tmp[0:nt], AF.Exp)
            for s_ in range(nt):
                e = mq + 2 + d0 + s_
                P.mm(O[mq], vt[e], PT[s_], s_ == 0, s_ == nt - 1)
            for s_ in range(nt):
                P.mm(Dn[mq], self.onesb, PT[s_], s_ == 0, s_ == nt - 1)
        rd = A([8, 128], F32)
        P.recip(rd, Dn)
        yo = A([8, 128], BF16)
        P.tt("dve", yo, O, rd, ALU.mult)
        P.dma(self.zy[8 + h], yo)
        self.release(m)

    def proj_pairs(self, colA, colB, funcA, dst):
        P = self.P
        m = self.mark()
        tmn = self.ring("pp_t", 2, [512], F32)
        obn = self.ring("pp_o", 2, [512], BF16)
        i0s = (0, 16, 528, 1040)
        for j in range(4):
            w = self.next_w()
            P.dma(w[0:KC, 0:256], self.w_in[:, colA + j * 256:colA + (j + 1) * 256].rearrange("(k p) c -> p k c", p=128), q="pool")
            P.dma(w[0:KC, 256:512], self.w_in[:, colB + j * 256:colB + (j + 1) * 256].rearrange("(k p) c -> p k c", p=128), q="pool")
            for a in range(2):
                ch = 2 * j + a
                for ci, (rhs_fn, n) in enumerate(self.conv_chunks()):
                    pa = self.pbank(self._pr % 4, (512,))
                    pb = self.pbank((self._pr + 1) % 4, (512,))
                    self._pr += 2
                    for k in range(KC):
                        P.mm(pa[0:n], w[k, a * 128:(a + 1) * 128], rhs_fn(k), k == 0, k == KC - 1)
                    for k in range(KC):
                        P.mm(pb[0:n], w[k, 256 + a * 128:256 + (a + 1) * 128], rhs_fn(k), k == 0, k == KC - 1)
                    tm, ob = tmn(), obn()
                    P.act(tm[0:n], pa[0:n], funcA)
                    P.tt("dve", ob[0:n], tm[0:n], pb[0:n], ALU.mult)
                    if ci == 0:
                        P.ts("dve", ob[0:n], ob[0:n], self.pc[0:1], None, ALU.mult)
                    if ci == 3:
                        P.ts("dve", ob[0:n], ob[0:n], self.pc[1:2], None, ALU.mult)
                    P.dma(dst[ch, :, i0s[ci]:i0s[ci] + n], ob[0:n])
        self.release(m)

    def sc_branch(self):
        P = self.P
        m = self.mark()
        A = self.alloc
        wt = A([8, 3], F32)
        P.dma(wt, self.scw)
        pn = self.ring("sc_p", 2, [1056], BF16)
        sbn = self.ring("sc_sb", 2, [1024], BF16)
        dn = self.ring("sc_d", 2, [3, 128], BF16)
        yon = self.ring("sc_y", 2, [1024], BF16)
        for ch in range(8):
            p, sb, d3, yo = pn(), sbn(), dn(), yon()
            P.dma(p, self.zp[ch])
            P.dma(sb, self.zsb[ch])
            for k in range(3):
                P.ts("pool", d3[k], self.identb, wt[ch, k:k + 1], None, ALU.mult)
            for th in range(2):
                pb = self.pbank(self._pr % 4, (512,))
                self._pr += 1
                for k in range(3):
                    o = 16 + th * 512 + k - 1
                    P.mm(pb, d3[k], p[o:o + 512], k == 0, k == 2)
                P.tt("dve", yo[th * 512:(th + 1) * 512], sb[th * 512:(th + 1) * 512], pb, ALU.mult)
            P.dma(self.zy[16 + ch], yo)
        self.release(m)

    def cf_branch(self):
        P = self.P
        m = self.mark()
        A = self.alloc
        wt = A([8, 31], F32)
        P.dma(wt, self.cfw)
        ln = A([2, 8], F32)
        P.dma(ln, self.cfln)
        un = self.ring("cf_u", 2, [1056], BF16)
        d31 = A([31, 128], BF16)
        yc = A([8, 1024], F32)
        ybn = self.ring("cf_yb", 2, [1024], BF16)
        ysn = self.ring("cf_ys", 2, [1024], BF16)
        sm = self.pbank(4, (1024,))
        sq = self.pbank(6, (1024,))
        for ch in range(8):
            u = un()
            P.dma(u, self.zu[ch])
            for k in range(31):
                P.ts("pool" if k % 2 else "dve", d31[k], self.identb, wt[ch, k:k + 1], None, ALU.mult)
            yb, ys = ybn(), ysn()
            for th in range(2):
                pb = self.pbank(self._pr % 4, (512,))
                self._pr += 1
                for k in range(31):
                    o = 16 + th * 512 + k - 15
                    P.mm(pb, d31[k], u[o:o + 512], k == 0, k == 30)
                P.act(yc[ch, th * 512:(th + 1) * 512], pb, AF.Copy)
                P.act(ys[th * 512:(th + 1) * 512], pb, AF.Square)
                P.copy("dve", yb[th * 512:(th + 1) * 512], pb)
            for th in range(2):
                P.mm(sm[th * 512:(th + 1) * 512], self.onesb, yb[th * 512:(th + 1) * 512], ch == 0, ch == 7)
                P.mm(sq[th * 512:(th + 1) * 512], self.onesb, ys[th * 512:(th + 1) * 512], ch == 0, ch == 7)
        mean, t2 = A([1024], F32), A([1024], F32)
        P.ts("dve", mean, sm, 1.0 / 1024, None, ALU.mult)
        P.tt("pool", t2, mean, mean, ALU.mult)
        P.stt("dve", t2, sq, 1.0 / 1024, t2, ALU.mult, ALU.subtract)
        self.rstd_chain(t2, t2)
        yon = self.ring("cf_yo", 2, [1024], BF16)
        for ch in range(8):
            P.tt("dve", yc[ch], yc[ch], mean, ALU.subtract)
            P.tt("pool", yc[ch], yc[ch], t2, ALU.mult)
            yo = yon()
            P.act(yo, yc[ch], AF.Silu, bias=ln[1, ch:ch + 1], scale=ln[0, ch:ch + 1])
            P.dma(self.zy[24 + ch], yo)
        self.release(m)

    def merge(self):
        P = self.P
        m = self.mark()
        A = self.alloc
        acc = A([4, 1024], F32)
        ybn = self.ring("mg_yb", 3, [4, 1024], BF16)
        gn = self.ring("mg_g", 3, [1024], BF16)
        tn = self.ring("mg_t", 2, [512], F32)
        mgn = self.ring("mg_o", 2, [1024], BF16)
        for cg in range(4):
            for i in range(4):
                w = self.load_w(self.w_bo[i][:, cg * 512:(cg + 1) * 512], nk=8)
                ybs = []
                for hf in range(2):
                    yb = ybn()
                    lo = i * 8 + hf * 4
                    P.dma(yb, V(self.zy.ap[lo:lo + 4].rearrange("c p t -> p c t"), self.zy[lo:lo + 4].regs))
                    ybs.append(yb)
                for cc in range(4):
                    g = gn()
                    P.dma(g, self.zgate[i * 16 + cg * 4 + cc])
                    for th in range(2):
                        pb = self.pbank(self._pr % 4, (512,))
                        self._pr += 1
                        for kk in range(8):
                            P.mm(pb, w[kk, cc * 128:(cc + 1) * 128], ybs[kk // 4][kk % 4, th * 512:(th + 1) * 512],
                                 kk == 0, kk == 7)
                        a = acc[cc, th * 512:(th + 1) * 512]
                        gv = g[th * 512:(th + 1) * 512]
                        if i == 0:
                            P.tt("dve", a, gv, pb, ALU.mult)
                        else:
                            t = tn()
                            P.tt("dve", t, gv, pb, ALU.mult)
                            P.tt("dve", a, a, t, ALU.add)
            for cc in range(4):
                mg = mgn()
                P.act(mg, acc[cc], AF.Copy)
                P.dma(self.zmg[cg, :, cc], mg)
        self.release(m)
        for cg in range(4):
            P.dma(self.hT[cg * 4:(cg + 1) * 4], self.zmg[cg])

    def dense_to_zyo(self, wsrc_fn, ngroups=4):
        P = self.P
        m = self.mark()
        yon = self.ring("wo_y", 3, [512], F32)
        for cg in range(ngroups):
            w = self.load_w(wsrc_fn(cg))

            def epi(cc, ci, pb, n, cg=cg):
                yo = yon()
                P.act(yo, pb, AF.Copy)
                P.dma(self.zyo[cg * 4 + cc, :, ci * 512:(ci + 1) * 512], yo)
            self.proj(w, KC, range(4), self.own_chunks(), epi)
        self.release(m)

    def postnorm_residual(self, gidx):
        P = self.P
        m = self.mark()
        yn = self.ring("pn_y", 2, [1024], F32)
        rstd = self.alloc([1024], F32)
        self.rstd_from_sq(None, KC, 1.0 / D, rstd, loader=lambda k, t: P.dma(t, self.zyo[k]), ring=yn)
        tn = self.ring("pn_t", 2, [1024], F32)
        for j in range(KC):
            y = yn()
            P.dma(y, self.zyo[j])
            t = tn()
            P.tt("dve", t, y, rstd, ALU.mult)
            P.stt("dve", self.xv(j), t, self.der[gidx, j:j + 1], self.xv(j), ALU.mult, ALU.add)
        self.release(m)

    def mlp(self):
        P = self.P
        self.norm_to_h(lambda k: self.der[1, k:k + 1], lambda k: self.modT[48 + k:49 + k])
        m = self.mark()
        tn = self.ring("ff_t", 2, [512], F32)
        un = self.ring("ff_u", 3, [512], BF16)
        for g in range(16):
            w = self.load_w(self.w_ff1[:, g * 512:(g + 1) * 512])

            def epi(cc, ci, pb, n, g=g):
                t, u = tn(), un()
                P.act(t, pb, AF.Relu)
                P.tt("dve", u, t, t, ALU.mult)
                P.dma(self.zuu[g * 4 + cc, :, ci * 512:(ci + 1) * 512], u)
            self.proj(w, KC, range(4), self.own_chunks(), epi)
        self.release(m)
        self.release(self.hT_mark)
        m = self.mark()
        upn = self.ring("ff_up", 3, [8, 1024], BF16)
        yon = self.ring("ff_y", 2, [512], F32)
        for cg in range(4):
            for pc_ in range(8):
                w = self.load_w(self.w_ff2[pc_ * 1024:(pc_ + 1) * 1024, cg * 512:(cg + 1) * 512], nk=8)
                up = upn()
                P.dma(up, V(self.zuu.ap[pc_ * 8:(pc_ + 1) * 8].rearrange("c p t -> p c t"), self.zuu[pc_ * 8:(pc_ + 1) * 8].regs))
                for cc in range(4):
                    for th in range(2):
                        pb = self.pbank(cc * 2 + th, (512,))
                        for kk in range(8):
                            P.mm(pb, w[kk, cc * 128:(cc + 1) * 128], up[kk, th * 512:(th + 1) * 512],
                                 pc_ == 0 and kk == 0, pc_ == 7 and kk == 7)
            for cc in range(4):
                for th in range(2):
                    yo = yon()
                    P.act(yo, self.pbank(cc * 2 + th, (512,)), AF.Copy)
                    P.dma(self.zyo[cg * 4 + cc, :, th * 512:(th + 1) * 512], yo)
        self.release(self.base_mark)
        self.postnorm_residual(3)


def _na_tile_table(rpb_l, first, last):
    specs = [(0, -2, 6), (1, -2, 5), (3, -2, 5), (6, -2, 5), (7, -3, 6)]
    u = np.arange(128) // 64
    kc = np.arange(128) % 64
    v = np.arange(128) // 64
    c = np.arange(128) % 64
    col_start = np.clip(c - 8, 0, 48)
    colok = (kc[:, None] >= col_start[None, :]) & (kc[:, None] < col_start[None, :] + 16)
    dc = kc[:, None] - c[None, :] + 15
    out = np.full((8, 128, 27, 128), NEG, np.float32)
    ti = 0
    for (m, d0, nt) in specs:
        for s in range(nt):
            delta = d0 + s
            if first:
                R0 = 0
            elif last:
                R0 = 48
            else:
                R0 = 16
            r = R0 + 2 * m + v
            kr = R0 + 2 * (m + delta) + u
            row_start = np.clip(r - 4, 0, 56)
            ok = (kr[:, None] >= row_start[None, :]) & (kr[:, None] < row_start[None, :] + 8)
            ok &= (kr[:, None] >= 0) & (kr[:, None] <= 63) & colok
            dr = kr[:, None] - r[None, :] + 7
            drc = np.clip(dr, 0, 14)
            dcc = np.clip(dc, 0, 30)
            g = rpb_l[:, drc, dcc]
            out[:, :, ti, :] = np.where(ok[None], g, np.float32(NEG))
            ti += 1
    assert ti == 27
    return out


def _to_fm(a):
    return np.ascontiguousarray(a.T.reshape(KC, 128, a.shape[0]).transpose(1, 0, 2))


def _from_fm(a):
    return np.ascontiguousarray(a.transpose(1, 0, 2).reshape(D, a.shape[2]).T)


class Host:
    def __init__(self, inp):
        f = np.float32
        self.inp = inp
        self.x = np.asarray(inp["x"], f)
        self.c = np.asarray(inp["c"], f)
        self.b_ada = np.ascontiguousarray(np.asarray(inp["b_ada"], f).reshape(DEPTH, 96, 128).transpose(0, 2, 1))
        gv = np.stack([inp["g_pre_mix"], inp["g_post_mix"], inp["g_pre_mlp"], inp["g_post_mlp"]], 1).astype(f)
        self.gvec = np.ascontiguousarray(gv.reshape(DEPTH, 4, KC, 128).transpose(0, 3, 1, 2))
        self.dec = np.ascontiguousarray(np.stack([inp["ret_decay_fwd"], inp["ret_decay_bwd"]], 1), f)
        self.scw = np.ascontiguousarray(np.asarray(inp["sc_conv"], f).reshape(DEPTH, 3, 8, 128).transpose(0, 3, 2, 1))
        self.cfw = np.ascontiguousarray(np.asarray(inp["cf_conv"], f).reshape(DEPTH, 31, 8, 128).transpose(0, 3, 2, 1))
        ln = np.stack([inp["cf_ln_g"], inp["cf_ln_b"]], 1).astype(f)
        self.cfln = np.ascontiguousarray(ln.reshape(DEPTH, 2, 8, 128).transpose(0, 3, 1, 2))
        i = np.arange(128, dtype=f)
        cst = np.zeros((128, 9, 128), f)
        cst[:, 0] = np.eye(128, dtype=f)
        cst[:, 1] = 1.0
        cst[:, 2] = np.abs(i[None, :] - i[:, None])
        cst[:, 3] = (i[None, :] >= i[:, None])
        cst[:, 4] = (i[:, None] > i[None, :])
        cst[:, 5] = i[None, :] + 1.0
        cst[:, 6] = 128.0 - i[None, :]
        self.cst = cst
        rpb = np.asarray(inp["na_rpb"], f)
        self.tabs = {}
        for kind in ((True, False), (False, False), (False, True)):
            self.tabs[kind] = [np.ascontiguousarray(_na_tile_table(rpb[l], *kind)) for l in range(DEPTH)]
        inv = (np.float32(10000.0) ** (-np.arange(64, dtype=f) / np.float32(64))).astype(f)
        self.rope = []
        self.pcst = []
        for j in range(NCORES):
            b, q = j // 4, j % 4
            off = q * T
            pos = np.arange(off, off + T).astype(f)
            ang = pos[None, :] * inv[:, None]
            cs = np.cos(ang).astype(f)
            sn = np.sin(ang).astype(f)
            rope = np.zeros((128, 2, T), f)
            rope[:64, 0] = cs
            rope[64:, 0] = cs
            rope[:64, 1] = -sn
            rope[64:, 1] = sn
            self.rope.append(rope)
            pc = np.zeros((128, 64), f)
            pc[:, 0] = 0.0 if q == 0 else 1.0
            pc[:, 1] = 0.0 if q == 3 else 1.0
            for s in range(8):
                sb, sq = s // 4, s % 4
                if sb == b and sq < q:
                    pc[:, 2 + s] = 1024.0 * (q - 1 - sq)
                    pc[:, 10 + s] = 1.0
                if sb == b and sq > q:
                    pc[:, 18 + s] = 1024.0 * (sq - q - 1)
                    pc[:, 26 + s] = 1.0
            p = np.arange(128, dtype=f)
            for cc in range(8):
                pc[:, 34 + cc] = 1023.0 - (cc * 128 + p)
                pc[:, 42 + cc] = cc * 128 + p
            self.pcst.append(pc)
        self.xT = [_to_fm(self.x[j // 4, (j % 4) * T:(j % 4 + 1) * T, :]) for j in range(NCORES)]

    def maps_M(self):
        cc = np.ascontiguousarray(self.c.reshape(2, KC, 128).transpose(2, 1, 0))
        wa = np.asarray(self.inp["w_ada"], np.float32)
        return [{"ccol2": cc, "w_ada_s": np.ascontiguousarray(wa[:, :, j * 1536:(j + 1) * 1536])} for j in range(NCORES)]

    def set_mod(self, res):
        mod = np.zeros((DEPTH, 2, 6 * D), np.float32)
        for j in range(NCORES):
            mp = np.asarray(res[j]["modp"])
            mod[:, :, j * 1536:(j + 1) * 1536] = mp.transpose(1, 3, 2, 0).reshape(DEPTH, 2, 1536)
        self.modc = np.ascontiguousarray(mod.reshape(DEPTH, 2, 96, 128).transpose(0, 1, 3, 2))

    def common(self, l, j):
        return {"xT": self.xT[j], "modc": self.modc[l, j // 4], "b_ada": self.b_ada[l], "gvec": self.gvec[l],
                "dec": self.dec[l], "cst": self.cst, "rope": self.rope[j], "pcst": self.pcst[j]}

    def maps_A(self, l):
        wkv = np.ascontiguousarray(np.asarray(self.inp["w_in"], np.float32)[l][:, 1024:3072])
        ms = []
        for j in range(NCORES):
            m = self.common(l, j)
            m["w_kv"] = wkv
            ms.append(m)
        return ms

    def maps_B(self, l, resA):
        inp = self.inp
        f = np.float32
        halos = [np.asarray(r["halo_out"]) for r in resA]
        Lall = np.ascontiguousarray(np.stack([np.asarray(r["Lout"], f) for r in resA], 0))
        w_in = np.ascontiguousarray(np.asarray(inp["w_in"], f)[l])
        w_bo = np.ascontiguousarray(np.stack([np.asarray(inp[k], f)[l] for k in ("w_ret_o", "w_na_o", "w_sc_o", "w_cf_o")], 0))
        w_o = np.ascontiguousarray(np.asarray(inp["w_o"], f)[l])
        w_ff1 = np.ascontiguousarray(np.asarray(inp["w_ff1"], f)[l])
        w_ff2 = np.ascontiguousarray(np.asarray(inp["w_ff2"], f)[l])
        ms = []
        for j in range(NCORES):
            q = j % 4
            m = self.common(l, j)
            halo = np.zeros_like(halos[j])
            if q > 0:
                halo[:, :, 0:256] = halos[j - 1][:, :, 256:512]
            if q < 3:
                halo[:, :, 256:512] = halos[j + 1][:, :, 0:256]
            m.update({"w_in": w_in, "nabias": self.tabs[(q == 0, q == 3)][l], "scw": self.scw[l], "cfw": self.cfw[l],
                      "cfln": self.cfln[l], "w_bo": w_bo, "w_o": w_o, "w_ff1": w_ff1, "w_ff2": w_ff2,
                      "halo": halo, "Lall": Lall})
            ms.append(m)
        return ms


def _ext_out(B, name, shape, dt=F32):
    t = B.nc.dram_tensor(name, list(shape), dt, kind="ExternalOutput")
    return t.ap()


def build_M():
    B = Builder("M")
    P = B.P
    cc = B.alloc([KC, 2], F32)
    P.dma(cc, B.ccol2)
    cact = B.alloc([KC, 2], BF16)
    P.act(cact, cc, AF.Silu)
    B.wb = [B.alloc([KC, 512], BF16) for _ in range(2)]
    B.wi = 0
    res = B.alloc([DEPTH, 12, 2], F32)
    out = _ext_out(B, "modp", [128, DEPTH, 12, 2])
    n = 0
    for l in range(DEPTH):
        for cg in range(3):
            w = B.load_w(B.w_ada_s[l][:, cg * 512:(cg + 1) * 512])
            for c4 in range(4):
                j = cg * 4 + c4
                pb = B.pbank(n % 8, (2,))
                n += 1
                for k in range(KC):
                    P.mm(pb, w[k, c4 * 128:(c4 + 1) * 128], cact[k], k == 0, k == KC - 1)
                P.copy("dve", res[l, j], pb)
    ov = V(out, [P.buf(("d", "modp", 0))])
    P.dma(ov, res)
    P.wait_all("sp", [ov])
    P.finalize()
    return B


def _common_front(B):
    B.setup()
    B.layer_tables()
    B.mod()
    B.norm_to_h(lambda k: B.der[0, k:k + 1], lambda k: B.modT[k:k + 1])


def build_A():
    B = Builder("A")
    P = B.P
    _common_front(B)
    B.scratch([("zk", [8, 128, T], BF16), ("zK", [8, 128, T], BF16), ("zv", [8, 128, T], BF16)])
    ho = _ext_out(B, "halo_out", [128, KC, 512], BF16)
    Lo = V(B.nc.dram_tensor("Lout", [2, 8, 128, 128], F32, kind="ExternalOutput").ap(), [])
    hv = V(ho[:, :, 0:256], [P.buf(("d", "halo_out", 0))])
    hv2 = V(ho[:, :, 256:512], [P.buf(("d", "halo_out", 1))])
    P.dma(hv, B.hT[0:KC, 0:256])
    P.dma(hv2, B.hT[0:KC, 768:1024])
    B.outs += [hv, hv2]
    for g2 in range(2):
        w = B.load_w(B.w_kv[:, g2 * 512:(g2 + 1) * 512])
        B.proj_rope(w, g2, B.zk, True)
    for g2 in range(2):
        w = B.load_w(B.w_kv[:, 1024 + g2 * 512:1024 + (g2 + 1) * 512])
        B.proj_tok(w, B.own_tok_srcs(), B.zv, g2 * 512)
    B.l_phase(Lo)
    P.wait_all("sp", B.outs)
    P.finalize()
    return B


def build_B(dbg=False, upto=99, dumps=(), only=None, scoped=False):
    B = Builder("B")
    P = B.P
    P.scoped = scoped
    P.label = "front"
    _common_front(B)
    B.hH = B.alloc([KC, 512], BF16)
    P.dma(B.hH, B.halo_in)
    B.scratch([
        ("zq", [8, 128, T], BF16), ("zk", [8, 128, T], BF16), ("zK", [8, 128, T], BF16), ("zv", [8, 128, T], BF16),
        ("zg", [8, 128, T], BF16), ("znq", [8, 128, T], BF16), ("znk", [8, 128, TE], BF16), ("znv", [12, 128, T], BF16),
        ("zsb", [8, 128, T], BF16), ("zp", [8, 128, 1056], BF16), ("zu", [8, 128, 1056], BF16),
        ("zgate", [64, 128, T], BF16), ("zy", [32, 128, T], BF16), ("zmg", [4, 128, 4, T], BF16),
        ("zyo", [16, 128, T], F32), ("zuu", [64, 128, T], BF16)])

    def wg(g):
        return B.load_w(B.w_in[:, g * 512:(g + 1) * 512])
    e0s = (0, 256, 768, 1280)
    ens = (256, 512, 512, 256)

    def st1():
        for g2 in range(2):
            B.proj_rope(wg(0 + g2), g2, B.zq, False, scale=128 ** -0.5)
        for g2 in range(2):
            B.proj_rope(wg(2 + g2), g2, B.zk, True)

    def st2():
        for g2 in range(2):
            B.proj_tok(wg(4 + g2), B.own_tok_srcs(), B.zv, g2 * 512)
        for g2 in range(2):
            B.proj_simple(wg(6 + g2), B.own_chunks(), AF.Silu,
                          lambda cc, ci, g2=g2: B.zg[g2 * 4 + cc, :, ci * 512:(ci + 1) * 512])
        for g2 in range(2):
            B.proj_simple(wg(8 + g2), B.own_chunks(), None,
                          lambda cc, ci, g2=g2: B.znq[g2 * 4 + cc, :, ci * 512:(ci + 1) * 512])

    def st3():
        for g2 in range(2):
            B.proj_simple(wg(10 + g2), B.ext_chunks(), None,
                          lambda cc, ci, g2=g2: B.znk[g2 * 4 + cc, :, e0s[ci]:e0s[ci] + ens[ci]])
        for g2 in range(2):
            B.proj_tok(wg(12 + g2), B.ext_tok_srcs(), B.znv, g2 * 512)
        for g2 in range(2):
            B.proj_simple(wg(14 + g2), B.own_chunks(), None,
                          lambda cc, ci, g2=g2: B.zsb[g2 * 4 + cc, :, ci * 512:(ci + 1) * 512])

    def st4():
        B.proj_pairs(8192, 9216, AF.Copy, B.zp)
        B.proj_pairs(11264, 10240, AF.Sigmoid, B.zu)

    def st5():
        for g in range(16):
            B.proj_simple(wg(24 + g), B.own_chunks(), AF.Sigmoid,
                          lambda cc, ci, g=g: B.zgate[g * 4 + cc, :, ci * 512:(ci + 1) * 512])

    def st6():
        B.release(B.hT_mark)
        for h in range(8):
            B.ret_head(h)
        B.release(B.base_mark)

    def st7():
        B.release(B.hT_mark)
        for h in range(8):
            B.na_head(h)
        B.release(B.base_mark)

    def st8():
        B.release(B.hT_mark)
        B.sc_branch()
        B.cf_branch()
        B.release(B.base_mark)

    def st9():
        B.release(B.hT_mark)
        B.merge()
        B.release(B.base_mark)

    def st10():
        B.dense_to_zyo(lambda cg: B.w_o[:, cg * 512:(cg + 1) * 512])
        B.postnorm_residual(2)

    def st11():
        B.mlp()
    stages = [st1, st2, st3, st4, st5, st6, st7, st8, st9, st10, st11]
    for i, st in enumerate(stages):
        if i + 1 > upto:
            break
        if only is not None and (i + 1) not in only:
            continue
        P.label = f"st{i + 1}"
        st()
        if (i + 1) == 9 and dbg:
            o = _ext_out(B, "dbg_mg", [128, KC, T], BF16)
            ov = V(o, [P.buf(("d", "dbg_mg", 0))])
            P.dma(ov, B.hT)
            B.outs.append(ov)
        if (i + 1) == 10 and dbg:
            o = _ext_out(B, "dbg_x1", [128, KC, T], F32)
            for k in range(KC):
                ov = V(o[:, k, :], [P.buf(("d", "dbg_x1", k))])
                P.dma(ov, B.xv(k))
                B.outs.append(ov)
    for nm in dumps:
        d = getattr(B, nm)
        o = _ext_out(B, "dbg_" + nm, list(d.shape), d.dt)
        ov = V(o, [P.buf(("d", "dbg_" + nm, 0))])
        P.dma(ov, d[0:d.shape[0]])
        B.outs.append(ov)
    xo = _ext_out(B, "xT_out", [128, KC, T])
    for k in range(KC):
        ov = V(xo[:, k, :], [P.buf(("d", "xT_out", k))])
        P.dma(ov, B.xv(k))
        B.outs.append(ov)
    P.wait_all("sp", B.outs)
    P.finalize()
    return B


_PROGS = {}


def _prog(kind, **kw):
    if kind not in _PROGS:
        _PROGS[kind] = {"M": build_M, "A": build_A, "B": build_B}[kind](**kw)
    return _PROGS[kind]


def _launch(B, maps):
    names = B.in_names
    res = run_bass_kernel_spmd(B.nc, [{k: m[k] for k in names} for m in maps], core_ids=list(range(NCORES)))
    return res.results


def kernel(**inputs):
    H = Host(inputs)
    H.set_mod(_launch(_prog("M"), H.maps_M()))
    for l in range(DEPTH):
        resA = _launch(_prog("A"), H.maps_A(l))
        resB = _launch(_prog("B"), H.maps_B(l, resA))
        H.xT = [np.asarray(r["xT_out"], np.float32) for r in resB]
    out = np.zeros((2, 4 * T, D), np.float32)
    for j in range(NCORES):
        out[j // 4, (j % 4) * T:(j % 4 + 1) * T, :] = _from_fm(H.xT[j])
    return out
```
